# Optimizing a Trainium2 kernel written in Bass

```python
import math
import jax, jax.numpy as jnp
from jax import lax
import numpy as np

D_MODEL = 1024
BATCH = 8
SEQ = 4096
DEPTH = 4

N_MIXERS = 3
N_NORMS = 7
NORM_EPS = 1e-6
D_FF = 2816
PLE_DIM = 256
CONV_WIDTH = 3
ATT_HEADS = 16
ATT_KV_HEADS = 2
ATT_HEAD_DIM = 64
WINDOW = 128
BLOCK = 128
REL_BUCKETS = 32
REL_MAX_DIST = 128
RET_HEADS = 4
RET_QK_DIM = D_MODEL // RET_HEADS
RET_V_DIM = 2 * RET_QK_DIM
RET_CHUNK = 128
ROPE_BASE = 10000.0

kernel_name = "interleaved_conv_swa_retention_macaron"


def _n_layers_of(kind):
    return (DEPTH - kind + N_MIXERS - 1) // N_MIXERS


def rmsnorm(x, g):
    xf = x.astype(jnp.float32)
    y = xf * lax.rsqrt(jnp.mean(xf * xf, axis=-1, keepdims=True) + NORM_EPS)
    return (y * g.astype(jnp.float32)).astype(x.dtype)


def swiglu(x, w_gu, w_down):
    a, b = jnp.split(x @ w_gu, 2, axis=-1)
    return (jax.nn.silu(a) * b) @ w_down


def short_conv_mixer(x, w_in, conv_w, w_out):
    S = x.shape[1]
    bgate, cgate, v = jnp.split(x @ w_in, 3, axis=-1)
    u = jnp.pad(cgate * v, ((0, 0), (CONV_WIDTH - 1, 0), (0, 0)))
    conv = sum(conv_w[j] * u[:, j:j + S] for j in range(CONV_WIDTH))
    return (bgate * conv) @ w_out


def rel_bucket(dist):
    max_exact = REL_BUCKETS // 2
    d = jnp.maximum(dist, 1).astype(jnp.float32)
    large = max_exact + (jnp.log(d / max_exact) / math.log(REL_MAX_DIST / max_exact)
                         * (REL_BUCKETS - max_exact)).astype(jnp.int32)
    large = jnp.minimum(large, REL_BUCKETS - 1)
    return jnp.where(dist < max_exact, dist, large)


def swa_mixer(x, w_qkv, sinks, w_o, rel_bias):
    Bsz, S, _ = x.shape
    nb = S // BLOCK
    G = ATT_HEADS // ATT_KV_HEADS
    q, k, v = jnp.split(x @ w_qkv, [ATT_HEADS * ATT_HEAD_DIM,
                                    (ATT_HEADS + ATT_KV_HEADS) * ATT_HEAD_DIM], axis=-1)
    q = q.reshape(Bsz, nb, BLOCK, ATT_KV_HEADS, G, ATT_HEAD_DIM)
    k = k.reshape(Bsz, nb, BLOCK, ATT_KV_HEADS, ATT_HEAD_DIM)
    v = v.reshape(Bsz, nb, BLOCK, ATT_KV_HEADS, ATT_HEAD_DIM)
    pad = ((0, 0), (1, 0), (0, 0), (0, 0), (0, 0))
    kb = jnp.concatenate([jnp.pad(k, pad)[:, :-1], k], axis=2)
    vb = jnp.concatenate([jnp.pad(v, pad)[:, :-1], v], axis=2)
    logits = jnp.einsum('bnqhgd,bnkhd->bnhgqk', q, kb).astype(jnp.float32)
    logits = logits * (ATT_HEAD_DIM ** -0.5)
    qi = jnp.arange(BLOCK)[:, None]
    kk = jnp.arange(2 * BLOCK)[None, :]
    dist = qi + BLOCK - kk
    in_window = (dist >= 0) & (dist < WINDOW)
    bias = rel_bias[rel_bucket(jnp.maximum(dist, 0))]
    bias = bias.transpose(2, 0, 1).reshape(ATT_KV_HEADS, G, BLOCK, 2 * BLOCK)
    blk_valid = (jnp.arange(nb)[:, None] > 0) | (kk >= BLOCK)
    mask = in_window[None] & blk_valid[:, None, :]
    logits = jnp.where(mask[None, :, None, None], logits + bias.astype(jnp.float32), -jnp.inf)
    sink = sinks.astype(jnp.float32).reshape(ATT_KV_HEADS, G)[None, None, :, :, None, None]
    m = jnp.maximum(jnp.max(logits, axis=-1, keepdims=True), sink)
    e = jnp.exp(logits - m)
    probs = e / (jnp.sum(e, axis=-1, keepdims=True) + jnp.exp(sink - m))
    out = jnp.einsum('bnhgqk,bnkhd->bnqhgd', probs.astype(vb.dtype), vb)
    return out.reshape(Bsz, S, ATT_HEADS * ATT_HEAD_DIM) @ w_o


def rotary(x, pos):
    half = x.shape[-1] // 2
    inv = ROPE_BASE ** (-jnp.arange(half, dtype=jnp.float32) / half)
    ang = pos[:, None].astype(jnp.float32) * inv[None, :]
    cos = jnp.cos(ang)[:, None, :]
    sin = jnp.sin(ang)[:, None, :]
    x1, x2 = x[..., :half], x[..., half:]
    return jnp.concatenate([x1 * cos - x2 * sin, x1 * sin + x2 * cos], axis=-1)


def retention_mixer(x, w_qkvg, w_o):
    Bsz, S, _ = x.shape
    H, dk, dv, C = RET_HEADS, RET_QK_DIM, RET_V_DIM, RET_CHUNK
    nc = S // C
    f32 = jnp.float32
    q, k, v, g = jnp.split(x @ w_qkvg, [H * dk, 2 * H * dk, 2 * H * dk + H * dv], axis=-1)
    pos = jnp.arange(S)
    q = rotary(q.reshape(Bsz, S, H, dk).astype(f32), pos)
    k = rotary(k.reshape(Bsz, S, H, dk).astype(f32), pos) * (dk ** -0.5)
    v = v.reshape(Bsz, S, H, dv).astype(f32)
    log_g = jnp.log(1.0 - 2.0 ** (-5.0 - jnp.arange(H, dtype=f32)))
    idx = jnp.arange(C, dtype=f32)
    diff = idx[:, None] - idx[None, :]
    decay_mask = jnp.where(diff >= 0, jnp.exp(log_g[:, None, None] * jnp.maximum(diff, 0.0)), 0.0)
    q_decay = jnp.exp(log_g[:, None] * (idx + 1.0))[..., None]
    k_decay = jnp.exp(log_g[:, None] * (C - 1.0 - idx))[..., None]
    chunk_decay = jnp.exp(log_g * C)[:, None, None]

    def to_chunks(t):
        return t.reshape(Bsz, nc, C, H, t.shape[-1]).transpose(1, 0, 3, 2, 4)

    def step(state, inp):
        qi, ki, vi = inp
        inner = jnp.einsum('bhqd,bhkd->bhqk', qi, ki) * decay_mask
        o = (jnp.einsum('bhqk,bhkv->bhqv', inner, vi)
             + jnp.einsum('bhqd,bhdv->bhqv', qi, state) * q_decay)
        state = state * chunk_decay + jnp.einsum('bhkd,bhkv->bhdv', ki * k_decay, vi)
        return state, o

    state0 = jnp.zeros((Bsz, H, dk, dv), f32)
    _, o = lax.scan(step, state0, (to_chunks(q), to_chunks(k), to_chunks(v)))
    o = o.transpose(1, 0, 3, 2, 4).reshape(Bsz, S, H, dv)
    mu = jnp.mean(o, axis=-1, keepdims=True)
    var = jnp.mean(jnp.square(o - mu), axis=-1, keepdims=True)
    o = ((o - mu) * lax.rsqrt(var + NORM_EPS)).reshape(Bsz, S, H * dv).astype(x.dtype)
    return (jax.nn.silu(g) * o) @ w_o


def setup_inputs(seed: int = 0) -> dict:
    key = jax.random.key(seed)
    ks = jax.random.split(key, 20)
    f32 = jnp.float32
    nA, nB, nC = _n_layers_of(0), _n_layers_of(1), _n_layers_of(2)

    def w(k, shape, fan_in):
        return jax.random.normal(k, shape, f32) * (fan_in ** -0.5)

    return {
        "x": jax.random.normal(ks[0], (BATCH, SEQ, D_MODEL), f32),
        "p": jax.random.normal(ks[1], (DEPTH, BATCH, SEQ, PLE_DIM), f32),
        "norm_g": 1.0 + 0.05 * jax.random.normal(ks[2], (DEPTH, N_NORMS, D_MODEL), f32),
        "ffn_w_gu": w(ks[3], (DEPTH, 2, D_MODEL, 2 * D_FF), D_MODEL),
        "ffn_w_down": w(ks[4], (DEPTH, 2, D_FF, D_MODEL), D_FF),
        "ple_w_proj": w(ks[5], (DEPTH, PLE_DIM, D_MODEL), PLE_DIM),
        "ple_w_gate": w(ks[6], (DEPTH, D_MODEL, D_MODEL), D_MODEL),
        "rel_bias": 0.5 * jax.random.normal(ks[7], (REL_BUCKETS, ATT_HEADS), f32),
        "conv_w_in": w(ks[8], (nA, D_MODEL, 3 * D_MODEL), D_MODEL),
        "conv_w": w(ks[9], (nA, CONV_WIDTH, D_MODEL), CONV_WIDTH),
        "conv_w_out": w(ks[10], (nA, D_MODEL, D_MODEL), D_MODEL),
        "swa_w_qkv": w(ks[11], (nB, D_MODEL, (ATT_HEADS + 2 * ATT_KV_HEADS) * ATT_HEAD_DIM), D_MODEL),
        "swa_sinks": 0.5 * jax.random.normal(ks[12], (nB, ATT_HEADS), f32),
        "swa_w_o": w(ks[13], (nB, ATT_HEADS * ATT_HEAD_DIM, D_MODEL), ATT_HEADS * ATT_HEAD_DIM),
        "ret_w_qkvg": w(ks[14], (nC, D_MODEL, 2 * RET_HEADS * RET_QK_DIM + 2 * RET_HEADS * RET_V_DIM), D_MODEL),
        "ret_w_o": w(ks[15], (nC, RET_HEADS * RET_V_DIM, D_MODEL), RET_HEADS * RET_V_DIM),
    }


def reference(x, p, norm_g, ffn_w_gu, ffn_w_down, ple_w_proj, ple_w_gate, rel_bias,
              conv_w_in, conv_w, conv_w_out, swa_w_qkv, swa_sinks, swa_w_o,
              ret_w_qkvg, ret_w_o):
    for i in range(DEPTH):
        kind, j = i % N_MIXERS, i // N_MIXERS
        g = norm_g[i]
        x = x + 0.5 * rmsnorm(swiglu(rmsnorm(x, g[0]), ffn_w_gu[i, 0], ffn_w_down[i, 0]), g[1])
        h = rmsnorm(x, g[2])
        if kind == 0:
            h = short_conv_mixer(h, conv_w_in[j], conv_w[j], conv_w_out[j])
        elif kind == 1:
            h = swa_mixer(h, swa_w_qkv[j], swa_sinks[j], swa_w_o[j], rel_bias)
        else:
            h = retention_mixer(h, ret_w_qkvg[j], ret_w_o[j])
        x = x + rmsnorm(h, g[3])
        x = x + 0.5 * rmsnorm(swiglu(rmsnorm(x, g[4]), ffn_w_gu[i, 1], ffn_w_down[i, 1]), g[5])
        gate = jax.nn.sigmoid(rmsnorm(x, g[6]) @ ple_w_gate[i])
        x = x + gate * (p[i] @ ple_w_proj[i])
    return x
```

```python
import math
import numpy as np
import concourse.bass as bass
import concourse.mybir as mybir
from concourse.bass_utils import run_bass_kernel_spmd

F32 = mybir.dt.float32
BF16 = mybir.dt.bfloat16
AF = mybir.ActivationFunctionType
ALU = mybir.AluOpType

D = 1024
T = 512
NCH = 4
DFF = 2816
NJ = 22
SEQ = 4096
DEPTH = 4
EPS = 1e-6
NRING = 5
NR = 22
NEG = -30000.0


class Buf:
    __slots__ = ("name", "w", "r", "excl")

    def __init__(self, name, excl=False):
        self.name = name
        self.w = None
        self.r = []
        self.excl = excl


class Op:
    __slots__ = ("eng", "fn", "deps", "sig", "cnt", "dsem", "dcnt")


COMPUTE = ("pe", "act", "dve", "pool")
ENGS = ("pe", "act", "dve", "pool", "sp")


class Prog:
    def __init__(self, nc):
        self.nc = nc
        self.ops = {e: [] for e in ENGS}
        self.dma_counts = {}
        self.bar = {e: [] for e in ENGS}

    def op(self, eng, fn, reads=(), writes=(), dsem=None, extra=()):
        o = Op()
        o.eng = eng
        o.fn = fn
        o.sig = False
        o.dsem = dsem
        o.cnt = 0
        o.dcnt = 0
        deps = {}
        is_dma = dsem is not None

        def add(d, raw):
            if d is None:
                return
            if (not is_dma) and d.dsem is None and d.eng == eng:
                if eng == "pe":
                    return
            deps[id(d)] = d

        for b in reads:
            add(b.w, True)
            if b.excl:
                for r in b.r:
                    add(r, False)
        for b in writes:
            add(b.w, False)
            for r in b.r:
                add(r, False)
        for d in extra:
            if d is not None:
                deps[id(d)] = d
        for d in self.bar[eng]:
            deps[id(d)] = d
        self.bar[eng] = []
        for b in writes:
            b.w = o
            b.r = []
        for b in reads:
            if b.w is o:
                continue
            if not is_dma:
                b.r = [r for r in b.r if not (r.eng == eng and r.dsem is None)]
            b.r.append(o)
        for d in deps.values():
            if d.dsem is None:
                d.sig = True
        o.deps = list(deps.values())
        if is_dma:
            c = self.dma_counts.get(dsem, 0) + 1
            self.dma_counts[dsem] = c
            o.dcnt = c
        self.ops[eng].append(o)
        return o

    def barrier(self):
        last = [self.ops[e][-1] for e in COMPUTE if self.ops[e]]
        for e in COMPUTE:
            for l in last:
                if l.eng != e:
                    self.bar[e].append(l)
        return last

    def emit(self):
        nc = self.nc
        esem = {e: nc.alloc_semaphore("es_" + e) for e in COMPUTE}
        dsem = {k: nc.alloc_semaphore("ds_" + k) for k in self.dma_counts}
        for e in COMPUTE:
            c = 0
            for o in self.ops[e]:
                if o.sig:
                    c += 1
                    o.cnt = c
        engobj = {"pe": "tensor", "act": "scalar", "dve": "vector", "pool": "gpsimd", "sp": "sync"}
        final = dict(self.dma_counts)

        def run(ename, eng):
            waited = {}
            for o in self.ops[ename]:
                need = {}
                for d in o.deps:
                    if d.dsem is not None:
                        key = ("d", d.dsem)
                        val = d.dcnt * 16
                    else:
                        key = ("e", d.eng)
                        val = d.cnt
                    if need.get(key, 0) < val:
                        need[key] = val
                for key, val in need.items():
                    if waited.get(key, 0) < val:
                        s = dsem[key[1]] if key[0] == "d" else esem[key[1]]
                        eng.wait_ge(s, val)
                        waited[key] = val
                ins = o.fn(eng)
                if o.dsem is not None:
                    ins.then_inc(dsem[o.dsem], 16)
                elif o.sig:
                    ins.then_inc(esem[ename], 1)
            if ename == "sp":
                for k, c in final.items():
                    if waited.get(("d", k), 0) < c * 16:
                        eng.wait_ge(dsem[k], c * 16)

        with nc.Block() as block:
            @block.sync
            def _(e):
                run("sp", e)

            @block.tensor
            def _(e):
                run("pe", e)

            @block.scalar
            def _(e):
                run("act", e)

            @block.vector
            def _(e):
                run("dve", e)

            @block.gpsimd
            def _(e):
                run("pool", e)


def _rel_bucket_np(dist):
    max_exact = 16
    d = np.maximum(dist, 1).astype(np.float32)
    large = max_exact + (np.log(d / max_exact) / math.log(128 / max_exact) * (32 - max_exact)).astype(np.int32)
    large = np.minimum(large, 31)
    return np.where(dist < max_exact, dist, large)


def _host_consts():
    c = {}
    c["ident"] = np.eye(128, dtype=np.float32)
    H, dk, C = 4, 256, 128
    half = dk // 2
    inv = (10000.0 ** (-np.arange(half, dtype=np.float32) / half)).astype(np.float32)
    pos = np.arange(SEQ, dtype=np.float32)
    ang = (pos[None, :] * inv[:, None]).astype(np.float32)
    cos = np.cos(ang).astype(np.float32)
    sin = np.sin(ang).astype(np.float32)
    gam = (1.0 - 2.0 ** (-5.0 - np.arange(H, dtype=np.float64)))
    j = (np.arange(SEQ) % C).astype(np.float64)
    rot = np.zeros((10, 128, SEQ), np.float32)
    rot[0] = cos
    rot[1] = sin
    for h in range(H):
        dq = gam[h] ** (j + 1.0)
        rot[2 + 2 * h] = (cos * dq[None, :]).astype(np.float32)
        rot[3 + 2 * h] = (sin * dq[None, :]).astype(np.float32)
    c["rot"] = rot
    kt = np.arange(C, dtype=np.float64)
    dm = np.zeros((128, H, 128), np.float32)
    skd = np.zeros((128, H), np.float32)
    for h in range(H):
        m = (gam[h] ** (-(kt + 1.0)) / 16.0)[:, None] * (kt[None, :] >= kt[:, None])
        dm[:, h, :] = m.astype(np.float32)
        skd[:, h] = (gam[h] ** (C - 1.0 - kt) / 16.0).astype(np.float32)
    c["dmask"] = dm
    dm0 = dm.copy()
    dm0[0, :, 0] = 0.0
    c["dmask0"] = dm0
    e00 = np.zeros((128, 128), np.float32)
    e00[0, 0] = 1.0 / 16.0
    c["e00"] = e00
    c["skd"] = skd
    c["gC"] = [float(gam[h] ** C) for h in range(H)]
    return c


def _bias_table(rel_bias):
    qi = np.arange(128)[None, :]
    out = np.empty((128, 2, 2, 8, 128), np.float32)
    for kb in range(2):
        kk = (np.arange(128) + kb * 128)[:, None]
        dist = qi + 128 - kk
        valid = (dist >= 0) & (dist < 128)
        bidx = _rel_bucket_np(np.maximum(dist, 0))
        gathered = rel_bias[bidx]
        for kv in range(2):
            for g in range(8):
                out[:, kv, kb, g, :] = np.where(valid, gathered[:, :, kv * 8 + g], np.float32(NEG))
    return out.reshape(128, 4096)


def build(n_tiles=8, layers=(0, 1, 2, 3), consts=None, subs=("f0", "mix", "f1", "ple")):
    nc = bass.Bass("TRN2", target_bir_lowering=False)
    P = Prog(nc)
    gC = consts["gC"]
    ntok = n_tiles * T

    def din(name, shape, dt=F32):
        return nc.dram_tensor(name, list(shape), dt, kind="ExternalInput").ap()

    x_d = din("x", [ntok, D])
    p_d = din("p", [DEPTH, ntok, 256])
    normg_d = din("norm_g", [DEPTH * 7 * 8, 128])
    wgu_d = din("ffn_w_gu", [DEPTH, 2, D, 2 * DFF])
    wdn_d = din("ffn_w_down", [DEPTH, 2, DFF, D])
    wpp_d = din("ple_w_proj", [DEPTH, 256, D])
    wpg_d = din("ple_w_gate", [DEPTH, D, D])
    cwin_d = din("conv_w_in", [2, D, 3 * D])
    cw_d = din("conv_w", [48, 128])
    cwout_d = din("conv_w_out", [2, D, D])
    sqkv_d = din("swa_w_qkv", [1, D, 1280])
    sink_d = din("swa_sinks", [1, 16])
    swo_d = din("swa_w_o", [1, D, D])
    rqkvg_d = din("ret_w_qkvg", [1, D, 6144])
    rwo_d = din("ret_w_o", [1, 2048, D])
    bm_d = din("c_bias", [128, 4096])
    rot_d = din("c_rot", [10, 128, SEQ])
    dmask_d = din("c_dmask", [128, 512])
    dmask0_d = din("c_dmask0", [128, 512])
    e00_d = din("c_e00", [128, 128])
    skd_d = din("c_skd", [128, 4])
    ident_d = din("c_ident", [128, 128])
    out_d = nc.dram_tensor("out", [ntok, D], F32, kind="ExternalOutput").ap()

    ring_items = {}
    R_items = {}
    ring_list = []
    R_list = []

    ring_groups = []

    def add_group(loads, stage_cols, items):
        its = []
        for (key, gidx, pieces) in items:
            ring_items[key] = len(ring_list)
            its.append((len(ring_list), gidx, pieces))
            ring_list.append(key)
        ring_groups.append((loads, stage_cols, its, items[0][0][1]))

    R_direct = {}

    def add_R(key, srcs, fold=None):
        if fold is None and key[0] != "ple":
            R_direct[key] = srcs
            return
        R_items[key] = len(R_list)
        R_list.append((srcs, fold, key[1]))

    for L in layers:
        kind, jj = L % 3, L // 3
        for f in range(2):
            w = wgu_d[L, f]
            gi = L * 7 + (0 if f == 0 else 4)
            for (j0, j1) in ((0, 8), (8, 16), (16, 22)):
                n = j1 - j0
                add_group([(0, w, j0 * 128, n * 128), (n * 128, w, DFF + j0 * 128, n * 128)], 2 * n * 128,
                          [(("ffn", L, f, j), gi, [(0, (j - j0) * 128, 128), (128, n * 128 + (j - j0) * 128, 128)]) for j in range(j0, j1)])
            for j in range(NJ):
                add_R(("ffn", L, f, j), [(wdn_d[L, f], j * 128, 128, 0)])
        if kind == 0:
            w = cwin_d[jj]
            for m0 in (0, 4):
                nm = 4
                loads = [(t * nm * 128, w, t * 1024 + m0 * 128, nm * 128) for t in range(3)]
                items = []
                for i in range(m0 * 3 // 2, (m0 + nm) * 3 // 2):
                    pieces = []
                    for s_ in (2 * i, 2 * i + 1):
                        m, t = s_ // 3, s_ % 3
                        pieces.append(((s_ % 2) * 128, t * nm * 128 + (m - m0) * 128, 128))
                    items.append((("mix", L, i), L * 7 + 2, pieces))
                add_group(loads, 3 * nm * 128, items)
            for m in range(8):
                add_R(("mix", L, m), [(cwout_d[jj], m * 128, 128, 0)])
        elif kind == 1:
            w = sqkv_d[0]
            items = []
            for i in range(4):
                pieces = []
                for hf, c in enumerate((2 * i, 2 * i + 1)):
                    pieces.append((hf * 128, c * 64, 64))
                    pieces.append((hf * 128 + 64, (8 + c) * 64, 64))
                items.append((("mix", L, i), L * 7 + 2, pieces))
            items.append((("mix", L, 4), L * 7 + 2, [(0, 1024, 256)]))
            add_group([(0, w, 0, 1280)], 1280, items)
            for c in range(8):
                add_R(("mix", L, c), [(swo_d[0], c * 64, 64, 0), (swo_d[0], (8 + c) * 64, 64, 64)])
        else:
            w = rqkvg_d[0]
            gi = L * 7 + 2
            add_group([(0, w, 0, 1024)], 1024, [(("mix", L, 6 * h + 0), gi, [(0, h * 256, 256)]) for h in range(4)])
            add_group([(0, w, 1024, 1024)], 1024, [(("mix", L, 6 * h + 1), gi, [(0, h * 256, 256)]) for h in range(4)])
            for base, t0 in ((2048, 2), (4096, 4)):
                for hp in range(2):
                    items = []
                    for h in (2 * hp, 2 * hp + 1):
                        for hv in range(2):
                            items.append((("mix", L, 6 * h + t0 + hv), gi, [(0, (h - 2 * hp) * 512 + hv * 256, 256)]))
                    add_group([(0, w, base + hp * 1024, 1024)], 1024, items)
            for kk in range(16):
                add_R(("mix", L, kk), [(rwo_d[0], kk * 128, 128, 0)])
        for k in range(8):
            add_R(("ple", L, k), [(wpg_d[L], k * 128, 128, 0)], fold=(L * 7 + 6, k))
        for k in range(2):
            add_R(("ple", L, 8 + k), [(wpp_d[L], k * 128, 128, 0)])

    wsA = nc.dram_tensor("wsA", [len(ring_list), 128, 2048], BF16, kind="Internal").ap()
    wsR = nc.dram_tensor("wsR", [len(R_list), 128, 1024], BF16, kind="Internal").ap()
    wsA_b = [Buf("wsA%d" % i) for i in range(len(ring_list))]
    wsR_b = [Buf("wsR%d" % i) for i in range(len(R_list))]

    SLAB_COLS = 52992
    slab_t = nc.alloc_sbuf_tensor("slab", [128, SLAB_COLS], F32)
    slab = slab_t.ap() if hasattr(slab_t, "ap") else slab_t
    ps_t = nc.alloc_psum_tensor("ps", [128, 8, 512], F32)
    ps_all = ps_t.ap() if hasattr(ps_t, "ap") else ps_t
    cur = [0]

    def alloc(nbytes):
        o = cur[0]
        cur[0] = o + ((nbytes + 63) // 64) * 64
        assert cur[0] <= SLAB_COLS * 4, "SBUF overflow %d" % cur[0]
        return o

    def view(off, shape, dt=F32):
        n = 1
        for s in shape:
            n *= s
        sz = 4 if dt == F32 else 2
        ap = slab[:, off // 4: (off + n * sz) // 4]
        if dt != F32:
            ap = ap.bitcast(dt)
        if len(shape) == 2:
            ap = ap.rearrange("p (a b) -> p a b", a=shape[0])
        elif len(shape) == 3:
            ap = ap.rearrange("p (a b c) -> p a b c", a=shape[0], b=shape[1])
        return ap

    def T_(shape, dt=F32):
        n = 1
        for s in shape:
            n *= s
        return view(alloc(n * (4 if dt == F32 else 2)), shape, dt)

    identf = T_([128])
    identb = T_([128], BF16)
    onesb = T_([128], BF16)
    gT = T_([28, 8])
    cwT = T_([2, 3, 8])
    expsink = T_([16])
    dmask = T_([4, 128])
    dmask0 = T_([4, 128])
    e00 = T_([128])
    skd = T_([4])
    mhalf = T_([4])
    sA = alloc(0)
    stg0 = T_([128])
    stg1 = T_([128])
    stg2 = T_([128])
    PRE0 = cur[0]
    x_sb = T_([NCH, D])
    xb = [Buf("x%d" % c) for c in range(NCH)]
    ring = [T_([8, 256], BF16) for _ in range(NRING)]
    ring_b = [Buf("ring%d" % i) for i in range(NRING)]
    Rs = [T_([1024], BF16) for _ in range(NR)]
    R_b = [Buf("R%d" % i) for i in range(NR)]
    gtab = [T_([D]) for _ in range(3)]
    gtab_b = [Buf("gtab%d" % i) for i in range(3)]
    state = T_([4, 2, 512])
    state_b = [[Buf("st%d%d" % (h, e)) for e in range(2)] for h in range(4)]
    ucarry = [T_([8, 2]) for _ in range(2)]
    ucarry_b = [Buf("ucar%d" % i) for i in range(2)]
    kcarry = T_([128], BF16)
    vcarry = T_([128], BF16)
    kvcarry_b = Buf("kvcar")
    cst_b = Buf("consts")
    junkP = T_([D], BF16)
    junkP_b = Buf("junkP")
    ssP = [T_([NCH]) for _ in range(2)]
    ssP_b = [[Buf("ssP%d%d" % (a, c)) for c in range(NCH)] for a in range(2)]
    xnTP = [T_([8, T], BF16) for _ in range(2)]
    xnTP_b = [[Buf("xnT%d%d" % (a, c)) for c in range(NCH)] for a in range(2)]
    xnbP = [T_([D], BF16) for _ in range(NCH)]
    xnbP_b = [Buf("xnb%d" % c) for c in range(NCH)]
    varP = [T_([NCH]) for _ in range(2)]
    varP_b = [[Buf("varP%d%d" % (a, c)) for c in range(NCH)] for a in range(2)]
    rstdP = [T_([NCH]) for _ in range(2)]
    rstdP_b = [[Buf("rstdP%d%d" % (a, c)) for c in range(NCH)] for a in range(2)]
    i00P = T_([4])
    i00_b = Buf("i00")
    pre = {"par": 0, "ss": [False] * NCH, "norm": [False] * NCH, "T": [False] * NCH, "pendT": None, "ahead": True}

    def pre_reset():
        pre["ss"] = [False] * NCH
        pre["norm"] = [False] * NCH
        pre["T"] = [False] * NCH
        pre["pendT"] = None
    ARENA0 = cur[0]

    ps = [ps_all[:, i, :] for i in range(8)]
    psb = [Buf("ps%d" % i, excl=True) for i in range(8)]
    psT = [ps_all[:, i, :].bitcast(BF16) for i in range(8)]
    a_ctr = [0]
    y_ctr = [0]

    def A_next():
        i = a_ctr[0] % 8
        a_ctr[0] += 1
        return i

    def Y_next():
        i = 2 * (y_ctr[0] % 4)
        y_ctr[0] += 1
        return i

    def ps2(i):
        return ps_all[:, i:i + 2, :].rearrange("p a b -> p (a b)")

    def dma(out, in_, dsem, reads, writes, extra=()):
        return P.op("sp", lambda e: e.dma_start(out=out, in_=in_), reads=reads, writes=writes, dsem=dsem, extra=extra)

    def act(out, in_, func, reads, writes, scale=1.0, accum_out=None):
        def fn(e):
            kw = {}
            if accum_out is not None:
                kw["accum_out"] = accum_out
            return e.activation(out=out, in_=in_, func=func, scale=scale, **kw)
        return P.op("act", fn, reads=reads, writes=writes)

    def tt(eng, out, in0, in1, op, reads, writes):
        return P.op(eng, lambda e: e.tensor_tensor(out=out, in0=in0, in1=in1, op=op), reads=reads, writes=writes)

    def stt(out, in0, scalar, in1, op0, op1, reads, writes):
        return P.op("dve", lambda e: e.scalar_tensor_tensor(out=out, in0=in0, scalar=scalar, in1=in1, op0=op0, op1=op1),
                    reads=reads, writes=writes)

    def ts(eng, out, in0, s1, s2, op0, op1, reads, writes):
        return P.op(eng, lambda e: e.tensor_scalar(out=out, in0=in0, scalar1=s1, scalar2=s2, op0=op0, op1=op1),
                    reads=reads, writes=writes)

    def mm_group(mms, reads, writes):
        def fn(e):
            ins = None
            for (o, l, r, st, sp) in mms:
                ins = e.matmul(o, lhsT=l, rhs=r, start=st, stop=sp)
            return ins
        return P.op("pe", fn, reads=reads, writes=writes)

    def tr_group(trs, reads, writes):
        def fn(e):
            ins = None
            for (o, i_, idn) in trs:
                ins = e.transpose(o, i_, idn)
            return ins
        return P.op("pe", fn, reads=reads, writes=writes)

    dma(identf, ident_d, "cst", [], [cst_b])
    dma(dmask.rearrange("p a b -> p (a b)"), dmask_d, "cst", [], [cst_b])
    dma(dmask0.rearrange("p a b -> p (a b)"), dmask0_d, "cst", [], [cst_b])
    dma(e00, e00_d, "cst", [], [cst_b])
    dma(skd, skd_d, "cst", [], [cst_b])
    dma(expsink, sink_d.partition_broadcast(128), "cst", [], [cst_b])
    stg_b = Buf("stg")
    dma(stg0[0:112, :], normg_d[0:112, :], "cst2", [], [stg_b])
    dma(stg1[0:112, :], normg_d[112:224, :], "cst2", [], [stg_b])
    dma(stg2[0:48, :], cw_d, "cst2", [], [stg_b])
    cst2_b = Buf("consts2")
    act(identb, identf, AF.Copy, [cst_b], [cst2_b])
    act(expsink, expsink, AF.Exp, [cst_b], [cst2_b])
    P.op("dve", lambda e: e.memset(onesb, 1.0), writes=[cst2_b])
    P.op("dve", lambda e: e.memset(mhalf, -0.5), writes=[cst2_b])
    tr_group([(ps[0][:, 0:112], stg0[0:112, :], identf[0:112, 0:112]),
              (ps[0][:, 112:224], stg1[0:112, :], identf[0:112, 0:112]),
              (ps[0][:, 256:304], stg2[0:48, :], identf[0:48, 0:48])], [stg_b, cst_b], [psb[0]])
    P.op("dve", lambda e: e.tensor_copy(out=gT.rearrange("p a b -> p (a b)"), in_=ps[0][:, 0:224]), reads=[psb[0]], writes=[cst2_b])
    P.op("dve", lambda e: e.tensor_copy(out=cwT.rearrange("p a b c -> p (a b c)"), in_=ps[0][:, 256:304]), reads=[psb[0]], writes=[cst2_b])

    SETB = 64 * 1024
    st_off = [PRE0, PRE0 + SETB]
    ob_off = PRE0 + 2 * SETB
    NOB = 4
    obuf = [view(ob_off + i * 4096, [8, 256], BF16) for i in range(NOB)]
    obuf_b = [Buf("obuf%d" % i) for i in range(NOB)]
    stset_b = [Buf("stset0"), Buf("stset1")]
    assert ob_off + NOB * 4096 <= SLAB_COLS * 4
    obc = [0]

    FG = set(layers[:2]) if len(layers) > 2 else set(layers)
    BGL = [L for L in layers if L not in FG]
    fg_groups = [g for g in ring_groups if g[3] in FG]
    fg_R = [n for n in range(len(R_list)) if R_list[n][2] in FG]
    for gi_, (loads, stage_cols, its, _lay) in enumerate(fg_groups):
        sset = gi_ % 2
        stg = view(st_off[sset], [8, stage_cols])
        for (dcol, w, c0, ncol) in loads:
            dma(stg[:, :, dcol:dcol + ncol], w[:, c0:c0 + ncol].rearrange("(k p) c -> p k c", p=128),
                "stg%d" % sset, [], [stset_b[sset]])
        for (ridx, gidx, pieces) in its:
            o = obc[0] % NOB
            obc[0] += 1
            eng = "dve" if (obc[0] % 2 == 0) else "pool"
            for (ocol, scol, n) in pieces:
                gb = gT[:, gidx, :].unsqueeze(2).to_broadcast([128, 8, n])
                tt(eng, obuf[o][:, :, ocol:ocol + n], stg[:, :, scol:scol + n], gb, ALU.mult, [stset_b[sset], cst2_b], [obuf_b[o]])
            dma(wsA[ridx], obuf[o].rearrange("p a b -> p (a b)"), "ob%d" % o, [obuf_b[o]], [wsA_b[ridx]])

    NST = 4
    stage = [view(st_off[0] + i * 4096, [1024]) for i in range(NST)]
    stage_b = [Buf("stage%d" % i) for i in range(NST)]
    fence_r = list(stset_b[0].r) + ([stset_b[0].w] if stset_b[0].w is not None else [])

    def pre_load(m_):
        n = fg_R[m_]
        srcs, fold, _lay = R_list[n]
        s_ = m_ % NST
        for (w, r0, nr, p0) in srcs:
            dma(stage[s_][p0:p0 + nr, :], w[r0:r0 + nr, :], "rstg%d" % s_, [], [stage_b[s_]], extra=fence_r if m_ < NST else ())

    def pre_proc(m_):
        n = fg_R[m_]
        srcs, fold, _lay = R_list[n]
        s_ = m_ % NST
        o = obc[0] % NOB
        obc[0] += 1
        ob2 = obuf[o].rearrange("p a b -> p (a b)")
        sc = 1.0 if fold is None else gT[:, fold[0], fold[1]:fold[1] + 1]
        act(ob2[:, 0:1024], stage[s_], AF.Copy, [stage_b[s_], cst2_b], [obuf_b[o]], scale=sc)
        dma(wsR[n], ob2[:, 0:1024], "ob%d" % o, [obuf_b[o]], [wsR_b[n]])

    nit = len(fg_R)
    for n in range(nit + 3):
        if n < nit:
            pre_load(n)
        if n >= 3:
            pre_proc(n - 3)
    last_pre = P.barrier()
    fence = []
    for b_ in obuf_b + stage_b + stset_b:
        for r_ in b_.r:
            if r_.dsem is not None:
                fence.append(r_)
        if b_.w is not None and b_.w.dsem is not None:
            fence.append(b_.w)
    for e_ in COMPUTE:
        P.bar[e_] += fence
    P.bar["sp"] += fence + list(last_pre)
    cur[0] = ARENA0
    P.op("dve", lambda e: e.memset(state.rearrange("p a b c -> p (a b c)"), 0.0), writes=[b for hb in state_b for b in hb])
    P.op("dve", lambda e: e.memset(ucarry[0].rearrange("p a b -> p (a b)"), 0.0), writes=[ucarry_b[0]])
    P.op("dve", lambda e: e.memset(ucarry[1].rearrange("p a b -> p (a b)"), 0.0), writes=[ucarry_b[1]])
    P.op("dve", lambda e: e.memset(kcarry, 0.0), writes=[kvcarry_b])
    P.op("dve", lambda e: e.memset(vcarry, 0.0), writes=[kvcarry_b])

    TOP = SLAB_COLS * 4
    BG_BYTES = 24 * 1024
    bgs = [view(TOP - BG_BYTES + i * 8192, [8, 256]) for i in range(2)]
    bgo = [view(TOP - 8192 + i * 4096, [8, 256], BF16) for i in range(2)]
    bgs_b = [Buf("bgs0"), Buf("bgs1")]
    bgo_b = [Buf("bgo0"), Buf("bgo1")]
    bgl = []
    bg_end = {}
    for L in BGL:
        for (loads, stage_cols, its, lay) in ring_groups:
            if lay != L:
                continue
            for (ridx, gidx, pieces) in its:
                pcs = []
                for (ocol, scol, n) in pieces:
                    for (dcol, w, c0, ncol) in loads:
                        if dcol <= scol < dcol + ncol:
                            pcs.append((ocol, w, c0 + (scol - dcol), n))
                            break
                assert len(pcs) == len(pieces)
                bgl.append(("A", ridx, gidx, pcs))
        for n in range(len(R_list)):
            if R_list[n][2] == L:
                bgl.append(("R", n))
        bg_end[L] = len(bgl)
    bg = {"nl": 0, "np": 0, "extra": (), "on": False, "stores": []}

    def bg_load(n):
        it = bgl[n]
        s_ = n % 2
        if it[0] == "A":
            for (ocol, w, c0, ncol) in it[3]:
                dma(bgs[s_][:, :, ocol:ocol + ncol], w[:, c0:c0 + ncol].rearrange("(k p) c -> p k c", p=128),
                    "bgs%d" % s_, [], [bgs_b[s_]], extra=bg["extra"])
        else:
            srcs, fold, _lay = R_list[it[1]]
            st2 = bgs[s_].rearrange("p a b -> p (a b)")
            for (w, r0, nr, p0) in srcs:
                dma(st2[p0:p0 + nr, 0:1024], w[r0:r0 + nr, :], "bgs%d" % s_, [], [bgs_b[s_]], extra=bg["extra"])

    def bg_proc(n):
        it = bgl[n]
        s_ = n % 2
        if it[0] == "A":
            _, ridx, gidx, pcs = it
            gb = gT[:, gidx, :].unsqueeze(2).to_broadcast([128, 8, 256])
            tt("pool", bgo[s_], bgs[s_], gb, ALU.mult, [bgs_b[s_], cst2_b], [bgo_b[s_]])
            st = dma(wsA[ridx], bgo[s_].rearrange("p a b -> p (a b)"), "bgo%d" % s_, [bgo_b[s_]], [wsA_b[ridx]])
        else:
            nR = it[1]
            srcs, fold, _lay = R_list[nR]
            st2 = bgs[s_].rearrange("p a b -> p (a b)")
            ob2 = bgo[s_].rearrange("p a b -> p (a b)")
            sc = 1.0 if fold is None else gT[:, fold[0], fold[1]:fold[1] + 1]
            act(ob2[:, 0:1024], st2[:, 0:1024], AF.Copy, [bgs_b[s_], cst2_b], [bgo_b[s_]], scale=sc)
            st = dma(wsR[nR], ob2[:, 0:1024], "bgo%d" % s_, [bgo_b[s_]], [wsR_b[nR]])
        bg["stores"] = (bg["stores"] + [st])[-2:]
        bg["fenced"] = False

    def bg_pump(k=1):
        if not bg["on"]:
            return
        for _ in range(k):
            if bg["np"] < bg["nl"] and (bg["nl"] - bg["np"] == 2 or bg["nl"] == len(bgl)):
                bg_proc(bg["np"])
                bg["np"] += 1
            if bg["nl"] < len(bgl) and bg["nl"] - bg["np"] < 2:
                bg_load(bg["nl"])
                bg["nl"] += 1

    def bg_drain():
        while bg["np"] < bg["nl"]:
            bg_proc(bg["np"])
            bg["np"] += 1

    def bg_require(upto):
        assert not bg.get("paused"), "background conversion deadline inside a phase whose arena overlaps its staging"
        while bg["np"] < upto:
            if bg["nl"] <= bg["np"]:
                bg_load(bg["nl"])
                bg["nl"] += 1
            bg_proc(bg["np"])
            bg["np"] += 1

    ring_seq = []
    for t in range(n_tiles):
        for L in layers:
            kind = L % 3
            nm = (12, 5, 24)[kind]
            if "f0" in subs:
                ring_seq += [("ffn", L, 0, j) for j in range(NJ)]
            if "mix" in subs:
                ring_seq += [("mix", L, i) for i in range(nm)]
            if "f1" in subs:
                ring_seq += [("ffn", L, 1, j) for j in range(NJ)]
    rstate = {"issued": 0, "consumed": 0}

    def ring_prefetch():
        while rstate["issued"] < len(ring_seq) and rstate["issued"] < rstate["consumed"] + NRING:
            n = rstate["issued"]
            s = n % NRING
            Lk = ring_seq[n][1]
            if Lk in bg_end and bg["np"] < bg_end[Lk]:
                bg_require(bg_end[Lk])
            i = ring_items[ring_seq[n]]
            dma(ring[s].rearrange("p a b -> p (a b)"), wsA[i], "ring%d" % s, [wsA_b[i]], [ring_b[s]])
            rstate["issued"] += 1

    def ring_next(key):
        n = rstate["consumed"]
        assert ring_seq[n] == key, (ring_seq[n], key)
        if rstate["issued"] <= n:
            ring_prefetch()
        s = n % NRING
        return ring[s], ring_b[s]

    def ring_done():
        rstate["consumed"] += 1
        ring_prefetch()

    def R_load(key, slot):
        if key in R_direct:
            for (w, r0, nr, p0) in R_direct[key]:
                P.op("pool", lambda e, w=w, r0=r0, nr=nr, p0=p0: e.dma_start(out=Rs[slot][p0:p0 + nr, :], in_=w[r0:r0 + nr, :]),
                     reads=[], writes=[R_b[slot]], dsem="Rg%d" % slot)
            return
        i = R_items[key]
        dma(Rs[slot], wsR[i], "R%d" % slot, [wsR_b[i]], [R_b[slot]])

    def rstd_from(ss, ssb_, scale, eps, n=1):
        var = T_([n])
        rs = T_([n])
        vb, rb = Buf("var"), Buf("rstd")
        ts("dve", var, ss, scale, eps, ALU.mult, ALU.add, [ssb_], [vb])
        tt("pool", rs, var, mhalf[:, 0:n], ALU.pow, [vb, cst2_b], [rb])
        return rs, rb

    ph = {}

    def new_phase(kind="x"):
        if bg["on"]:
            assert cur[0] <= TOP - BG_BYTES, "arena overlaps background staging: %d" % cur[0]
        pause = kind in ("swa", "ret", "tok0")
        if pause:
            bg_drain()
        last = P.barrier()
        if pause and not bg.get("fenced", True):
            for e_ in COMPUTE:
                P.bar[e_] += list(bg["stores"])
            bg["fenced"] = True
        bg["on"] = (bg["np"] < len(bgl)) and not pause
        bg["paused"] = pause
        bg["extra"] = tuple(last)
        cur[0] = ARENA0
        ph.clear()
        return last

    def ahead_square(c):
        a = pre["par"]
        act(junkP, x_sb[:, c, :], AF.Square, [xb[c]], [junkP_b, ssP_b[a][c]], accum_out=ssP[a][:, c:c + 1])
        pre["ss"][c] = True

    def ahead_norm(c):
        a = pre["par"]
        ts("dve", varP[a][:, c:c + 1], ssP[a][:, c:c + 1], 1.0 / D, EPS, ALU.mult, ALU.add, [ssP_b[a][c]], [varP_b[a][c]])
        tt("pool", rstdP[a][:, c:c + 1], varP[a][:, c:c + 1], mhalf[:, 0:1], ALU.pow, [varP_b[a][c], cst2_b], [rstdP_b[a][c]])
        if c % 2 == 0:
            act(xnbP[c], x_sb[:, c, :], AF.Copy, [xb[c], rstdP_b[a][c]], [xnbP_b[c]], scale=rstdP[a][:, c:c + 1])
        else:
            P.op("dve", lambda e, c=c, a=a: e.tensor_scalar(out=xnbP[c], in0=x_sb[:, c, :], scalar1=rstdP[a][:, c:c + 1], scalar2=None,
                                                           op0=ALU.mult), reads=[xb[c], rstdP_b[a][c]], writes=[xnbP_b[c]])
        pre["norm"][c] = True

    def emit_T(c):
        a = pre["par"]
        bk = A_next()
        tr_group([(psT[bk][:, k * 128:(k + 1) * 128], xnbP[c][:, k * 128:(k + 1) * 128], identb) for k in range(8)],
                 [xnbP_b[c], cst2_b], [psb[bk]])
        if c % 2 == 0:
            P.op("dve", lambda e, bk=bk, c=c, a=a: e.tensor_copy(out=xnTP[a][:, :, c * 128:(c + 1) * 128],
                                                                in_=psT[bk].rearrange("p (k t) -> p k t", k=8)),
                 reads=[psb[bk]], writes=[xnTP_b[a][c]])
        else:
            P.op("act", lambda e, bk=bk, c=c, a=a: e.activation(out=xnTP[a][:, :, c * 128:(c + 1) * 128],
                                                               in_=psT[bk].rearrange("p (k t) -> p k t", k=8), func=AF.Copy),
                 reads=[psb[bk]], writes=[xnTP_b[a][c]])
        pre["T"][c] = True

    def chunk_final(c):
        if not pre["ahead"]:
            if pre.get("io") is not None:
                tl = pre["io"]
                dma(out_d[tl * T + c * 128: tl * T + (c + 1) * 128, :], x_sb[:, c, :], "xio%d" % c, [xb[c]], [])
                if tl + 1 < n_tiles:
                    dma(x_sb[:, c, :], x_d[(tl + 1) * T + c * 128: (tl + 1) * T + (c + 1) * 128, :], "xio%d" % c, [], [xb[c]])
            return
        if pre["pendT"] is not None:
            emit_T(pre["pendT"])
        ahead_square(c)
        ahead_norm(c)
        pre["pendT"] = c

    def phase_finish():
        if pre["ahead"] and pre["pendT"] is not None:
            emit_T(pre["pendT"])
        pre["pendT"] = None

    def prenorm():
        a = pre["par"]
        for c in range(NCH):
            if not pre["ss"][c]:
                ahead_square(c)
        for c in range(NCH):
            if not pre["norm"][c]:
                ahead_norm(c)
        for c in range(NCH):
            if not pre["T"][c]:
                emit_T(c)
        pre["par"] = a ^ 1
        pre_reset()
        return xnTP[a], xnTP_b[a]

    def postnorm(c, yb, gt, gt_b, half):
        if "pn" not in ph:
            ph["pn"] = (junkP, junkP_b, [T_([D]) for _ in range(2)], [Buf("ptmp0"), Buf("ptmp1")], [0])
        junk, jb, tmps_, tbs_, ctr_ = ph["pn"]
        ssy = T_([1])
        sb_ = Buf("ssy")
        y = ps2(yb)
        act(junk, y, AF.Square, [psb[yb], psb[yb + 1]], [jb, sb_], accum_out=ssy[:, 0:1])
        if half:
            rs, rb = rstd_from(ssy, sb_, 4.0 / D, 4.0 * EPS)
        else:
            rs, rb = rstd_from(ssy, sb_, 1.0 / D, EPS)
        tmp = tmps_[ctr_[0] % len(tmps_)]
        tb = tbs_[ctr_[0] % len(tmps_)]
        ctr_[0] += 1
        stt(tmp, y, rs[:, 0:1], gt, ALU.mult, ALU.mult, [psb[yb], psb[yb + 1], rb, gt_b], [tb])
        tt("dve", x_sb[:, c, :], x_sb[:, c, :], tmp, ALU.add, [xb[c], tb], [xb[c]])
        chunk_final(c)

    def out_proj(c, actT, actT_bufs, nk, half, gt, gt_b, split=None):
        yb = Y_next()
        bufs = list(actT_bufs) if len(actT_bufs) == nk else [actT_bufs[0]] * nk
        mm = lambda hf, k: (ps[yb + hf], actT[:, k, c * 128:(c + 1) * 128], Rs[k][:, hf * 512:(hf + 1) * 512], k == 0, k == nk - 1)
        if split:
            mm_group([mm(0, k) for k in range(split)], list(dict.fromkeys(bufs[:split])) + [R_b[k] for k in range(split)], [psb[yb], psb[yb + 1]])
            mm_group([mm(0, k) for k in range(split, nk)] + [mm(1, k) for k in range(nk)],
                     list(dict.fromkeys(bufs)) + [R_b[k] for k in range(nk)], [psb[yb], psb[yb + 1]])
        else:
            mm_group([mm(hf, k) for hf in range(2) for k in range(nk)], list(dict.fromkeys(bufs)) + [R_b[k] for k in range(nk)],
                     [psb[yb], psb[yb + 1]])
        postnorm(c, yb, gt, gt_b, half)

    def gtab_load(L, which, slot):
        dma(gtab[slot], normg_d.rearrange("(l i k) p -> (l i) (k p)", i=7, k=8)[L * 7 + which: L * 7 + which + 1, :].partition_broadcast(128),
            "gt%d" % slot, [], [gtab_b[slot]])

    def ffn(L, f):
        new_phase()
        slot = 0 if f == 0 else 2
        gtab_load(L, 1 if f == 0 else 5, slot)
        xnT, xnT_b = prenorm()
        for j in range(NJ):
            R_load(("ffn", L, f, j), j)
        hT = T_([NJ, T], BF16)
        hT_b = [Buf("hT%d" % j) for j in range(NJ)]
        sg = [T_([T]) for _ in range(2)]
        sg_b = [Buf("sg0"), Buf("sg1")]
        for j in range(NJ):
            w, wb = ring_next(("ffn", L, f, j))
            bgt, bu = A_next(), A_next()
            mm_group([(ps[bgt], w[:, k, 0:128], xnT[:, k, :], k == 0, k == 7) for k in range(8)], [wb] + xnT_b, [psb[bgt]])
            mm_group([(ps[bu], w[:, k, 128:256], xnT[:, k, :], k == 0, k == 7) for k in range(8)], [wb] + xnT_b, [psb[bu]])
            ring_done()
            i = j % 2
            act(sg[i], ps[bgt], AF.Silu, [psb[bgt]], [sg_b[i]])
            tt("dve", hT[:, j, :], ps[bu], sg[i], ALU.mult, [psb[bu], sg_b[i]], [hT_b[j]])
            if j % 3 != 2:
                bg_pump(1)
        for c in range(NCH):
            out_proj(c, hT, hT_b, NJ, True, gtab[slot], gtab_b[slot], split=(16 if c == 0 else None))
            bg_pump(2)
        phase_finish()

    def ple(L, tile):
        last = new_phase()
        if L in bg_end and bg["np"] < bg_end[L]:
            bg_require(bg_end[L])
        for k in range(10):
            R_load(("ple", L, k), k)
        p_sb = T_([NCH, 256])
        p_b = Buf("p")
        dma(p_sb, p_d[L, tile * T:(tile + 1) * T, :].rearrange("(c p) f -> p c f", p=128), "pld", [], [p_b], extra=last)
        xnT, xnT_b = prenorm()
        pbf = T_([NCH, 256], BF16)
        pbf_b = Buf("pbf")
        P.op("dve", lambda e: e.tensor_copy(out=pbf, in_=p_sb), reads=[p_b], writes=[pbf_b])
        bk = A_next()
        tr_group([(psT[bk][:, (c * 2 + k) * 128:(c * 2 + k + 1) * 128], pbf[:, c, k * 128:(k + 1) * 128], identb)
                  for c in range(NCH) for k in range(2)], [pbf_b, cst2_b], [psb[bk]])
        pT = T_([2, T], BF16)
        pT_b = Buf("pT")
        for c in range(NCH):
            P.op("act", lambda e, c=c: e.activation(out=pT[:, :, c * 128:(c + 1) * 128],
                                                    in_=psT[bk][:, c * 256:(c + 1) * 256].rearrange("p (k t) -> p k t", k=2),
                                                    func=AF.Copy),
                 reads=[psb[bk]], writes=[pT_b])
        sgs = [T_([D]) for _ in range(2)]
        sgs_b = [Buf("sgs0"), Buf("sgs1")]
        tmps = [T_([D]) for _ in range(2)]
        tmps_b = [Buf("tmps0"), Buf("tmps1")]
        for c in range(NCH):
            yg = Y_next()
            mms = []
            for hf in range(2):
                for k in range(8):
                    mms.append((ps[yg + hf], xnT[:, k, c * 128:(c + 1) * 128], Rs[k][:, hf * 512:(hf + 1) * 512], k == 0, k == 7))
            mm_group(mms, xnT_b + [R_b[k] for k in range(8)], [psb[yg], psb[yg + 1]])
            yp = Y_next()
            mms = []
            for hf in range(2):
                for k in range(2):
                    mms.append((ps[yp + hf], pT[:, k, c * 128:(c + 1) * 128], Rs[8 + k][:, hf * 512:(hf + 1) * 512], k == 0, k == 1))
            mm_group(mms, [pT_b, R_b[8], R_b[9]], [psb[yp], psb[yp + 1]])
            i = c % 2
            act(sgs[i], ps2(yg), AF.Sigmoid, [psb[yg], psb[yg + 1]], [sgs_b[i]])
            tt("dve", tmps[i], ps2(yp), sgs[i], ALU.mult, [psb[yp], psb[yp + 1], sgs_b[i]], [tmps_b[i]])
            tt("dve", x_sb[:, c, :], x_sb[:, c, :], tmps[i], ALU.add, [xb[c], tmps_b[i]], [xb[c]])
            chunk_final(c)
            bg_pump(1)
        phase_finish()

    def conv(L):
        jj = L // 3
        new_phase()
        gtab_load(L, 3, 1)
        xnT, xnT_b = prenorm()
        for m in range(8):
            R_load(("mix", L, m), m)
        u = T_([8, T + 2])
        u_b = [Buf("u%d" % m) for m in range(8)]
        zT = T_([8, T], BF16)
        zT_b = [Buf("zTc%d" % m) for m in range(8)]
        Csb = [T_([T]) for _ in range(2)]
        Csb_b = [Buf("Csb0"), Buf("Csb1")]
        acc = [T_([T]) for _ in range(2)]
        acc_b = [Buf("acc0"), Buf("acc1")]
        P.op("act", lambda e: e.activation(out=u[:, :, 0:2], in_=ucarry[jj], func=AF.Copy), reads=[ucarry_b[jj]], writes=u_b)
        wcur = None
        for m in range(8):
            banks = []
            for t in range(3):
                s = 3 * m + t
                if s % 2 == 0:
                    wcur = ring_next(("mix", L, s // 2))
                w, wb = wcur
                hfc = s % 2
                bk = A_next()
                mm_group([(ps[bk], w[:, k, hfc * 128:(hfc + 1) * 128], xnT[:, k, :], k == 0, k == 7) for k in range(8)],
                         [wb] + xnT_b, [psb[bk]])
                if s % 2 == 1:
                    ring_done()
                banks.append(bk)
            bB, bC, bv = banks
            i = m % 2
            act(Csb[i], ps[bC], AF.Copy, [psb[bC]], [Csb_b[i]])
            tt("dve", u[:, m, 2:T + 2], ps[bv], Csb[i], ALU.mult, [psb[bv], Csb_b[i]], [u_b[m]])
            act(acc[i], u[:, m, 2:T + 2], AF.Copy, [u_b[m], cst2_b], [acc_b[i]], scale=cwT[:, jj, 2, m:m + 1])
            stt(acc[i], u[:, m, 1:T + 1], cwT[:, jj, 1, m:m + 1], acc[i], ALU.mult, ALU.add, [u_b[m], acc_b[i], cst2_b], [acc_b[i]])
            stt(acc[i], u[:, m, 0:T], cwT[:, jj, 0, m:m + 1], acc[i], ALU.mult, ALU.add, [u_b[m], acc_b[i], cst2_b], [acc_b[i]])
            tt("dve", zT[:, m, :], ps[bB], acc[i], ALU.mult, [psb[bB], acc_b[i]], [zT_b[m]])
            bg_pump(1)
        P.op("act", lambda e: e.activation(out=ucarry[jj], in_=u[:, :, T:T + 2], func=AF.Copy), reads=u_b, writes=[ucarry_b[jj]])
        ph["pn"] = (junkP, junkP_b, [T_([D])], [Buf("ptmp0")], [0])
        for c in range(NCH):
            out_proj(c, zT, zT_b, 8, False, gtab[1], gtab_b[1], split=(5 if c == 0 else None))
        phase_finish()

    def swa(L, tile):
        last = new_phase("swa")
        gtab_load(L, 3, 1)
        bm_sb = T_([2, 2, 1024])
        bm_b = Buf("bm")
        dma(bm_sb.rearrange("p a b c -> p (a b c)"), bm_d, "bml", [], [bm_b], extra=last)
        xnT, xnT_b = prenorm()
        for c in range(8):
            R_load(("mix", L, c), c)
        qT = T_([8, T], BF16)
        qT_b = Buf("qT")
        kT = T_([T + 128], BF16)
        kT_b = Buf("kT")
        v_sb = T_([5, 128], BF16)
        v_b = Buf("v")
        oT = T_([8, T], BF16)
        oT_b = [Buf("oT%d" % b) for b in range(NCH)]
        P.op("act", lambda e: e.activation(out=kT[:, 0:128], in_=kcarry, func=AF.Copy), reads=[kvcarry_b], writes=[kT_b])
        P.op("act", lambda e: e.activation(out=v_sb[:, 0, :], in_=vcarry, func=AF.Copy), reads=[kvcarry_b], writes=[v_b])
        for i in range(4):
            w, wb = ring_next(("mix", L, i))
            for hfc in range(2):
                c = 2 * i + hfc
                bk = A_next()
                mm_group([(ps[bk], w[:, k, hfc * 128:(hfc + 1) * 128], xnT[:, k, :], k == 0, k == 7) for k in range(8)],
                         [wb] + xnT_b, [psb[bk]])
                act(qT[:, c, :], ps[bk], AF.Copy, [psb[bk]], [qT_b], scale=0.125)
            ring_done()
        w, wb = ring_next(("mix", L, 4))
        bk = A_next()
        mm_group([(ps[bk], w[:, k, 0:128], xnT[:, k, :], k == 0, k == 7) for k in range(8)], [wb] + xnT_b, [psb[bk]])
        act(kT[:, 128:T + 128], ps[bk], AF.Copy, [psb[bk]], [kT_b])
        bk = A_next()
        mms = []
        for b in range(NCH):
            for k in range(8):
                mms.append((ps[bk][:, b * 128:(b + 1) * 128], xnT[:, k, b * 128:(b + 1) * 128], w[:, k, 128:256], k == 0, k == 7))
        mm_group(mms, [wb] + xnT_b, [psb[bk]])
        ring_done()
        P.op("dve", lambda e, bk=bk: e.tensor_copy(out=v_sb[:, 1:5, :], in_=ps[bk].rearrange("p (b d) -> p b d", b=4)),
             reads=[psb[bk]], writes=[v_b])
        P.op("act", lambda e: e.activation(out=kcarry, in_=kT[:, T:T + 128], func=AF.Copy), reads=[kT_b], writes=[kvcarry_b])
        P.op("act", lambda e: e.activation(out=vcarry, in_=v_sb[:, 4, :], func=AF.Copy), reads=[v_b], writes=[kvcarry_b])
        E = [[[T_([T], BF16) for _ in range(2)] for _ in range(2)] for _ in range(3)]
        E_b = [[[Buf("E%d%d%d" % (p_, a_, b_)) for b_ in range(2)] for a_ in range(2)] for p_ in range(3)]
        tmp = [T_([T]) for _ in range(2)]
        tmp_b = [Buf("stmp0"), Buf("stmp1")]
        rden = [T_([1024]) for _ in range(2)]
        rden_b = [[Buf("rden%d%d" % (p_, k_)) for k_ in range(2)] for p_ in range(2)]
        ph["pn"] = (junkP, junkP_b, [T_([D])], [Buf("ptmp0")], [0])
        tcs = [0]
        iters = [(b, kv) for b in range(NCH) for kv in range(2)]

        def kbs_of(b):
            return [1] if (tile == 0 and b == 0) else [0, 1]

        def s1(i):
            b, kv = iters[i]
            par = i % 3
            rows = slice(kv * 64, kv * 64 + 64)
            for kb in kbs_of(b):
                for hf in range(2):
                    bk = A_next()
                    mm_group([(ps[bk], kT[rows, (b + kb) * 128:(b + kb + 1) * 128],
                               qT[rows, hf * 4:(hf + 1) * 4, b * 128:(b + 1) * 128], True, True)],
                             [kT_b, qT_b], [psb[bk]])
                    ti = tcs[0] % 2
                    tcs[0] += 1
                    tt("dve", tmp[ti], ps[bk], bm_sb[:, kv, kb, hf * 512:(hf + 1) * 512], ALU.add, [psb[bk], bm_b], [tmp_b[ti]])
                    act(E[par][kb][hf], tmp[ti], AF.Exp, [tmp_b[ti]], [E_b[par][kb][hf]])

        def s2a(i):
            b, kv = iters[i]
            par = i % 3
            kbs = kbs_of(b)
            rows = slice(kv * 64, kv * 64 + 64)
            Ei, Ei_b = E[par], E_b[par]
            rd, rd_b = rden[(i // 2) % 2], rden_b[(i // 2) % 2][kv]
            yd = Y_next()
            mms = []
            for hf in range(2):
                for kb in kbs:
                    mms.append((ps[yd + hf], onesb, Ei[kb][hf], kb == kbs[0], kb == kbs[-1]))
            mm_group(mms, [cst2_b] + [Ei_b[kb][hf] for kb in kbs for hf in range(2)], [psb[yd], psb[yd + 1]])
            es = expsink[rows, kv * 8:(kv + 1) * 8].unsqueeze(2).to_broadcast([64, 8, 128])
            tt("dve", rd[rows, :].rearrange("p (g q) -> p g q", g=8), ps2(yd)[rows, :].rearrange("p (g q) -> p g q", g=8), es,
               ALU.add, [psb[yd], psb[yd + 1], cst2_b], [rd_b])
            act(rd[rows, :], rd[rows, :], AF.Ln, [rd_b], [rd_b])
            act(rd[rows, :], rd[rows, :], AF.Exp, [rd_b], [rd_b], scale=-1.0)

        def s2b(i):
            b, kv = iters[i]
            par = i % 3
            kbs = kbs_of(b)
            rows = slice(kv * 64, kv * 64 + 64)
            Ei, Ei_b = E[par], E_b[par]
            rd, rd_b = rden[(i // 2) % 2], rden_b[(i // 2) % 2][kv]
            yo = Y_next()
            mms = []
            for hf in range(2):
                for kb in kbs:
                    mms.append((ps[yo + hf], v_sb[:, b + kb, :], Ei[kb][hf], kb == kbs[0], kb == kbs[-1]))
            mm_group(mms, [v_b] + [Ei_b[kb][hf] for kb in kbs for hf in range(2)], [psb[yo], psb[yo + 1]])
            tt("dve", oT[rows, :, b * 128:(b + 1) * 128], ps2(yo)[rows, :].rearrange("p (g q) -> p g q", g=8),
               rd[rows, :].rearrange("p (g q) -> p g q", g=8), ALU.mult, [psb[yo], psb[yo + 1], rd_b], [oT_b[b]])

        n_it = len(iters)
        s1(0)
        s1(1)
        s2a(0)
        for i in range(n_it):
            if i + 2 < n_it:
                s1(i + 2)
            if i + 1 < n_it:
                s2a(i + 1)
            s2b(i)
            if iters[i][1] == 1:
                b_ = iters[i][0]
                out_proj(b_, oT, [oT_b[b_]], 8, False, gtab[1], gtab_b[1])
        phase_finish()

    def ret_tok0(L):
        last = new_phase("tok0")
        a = pre["par"]
        if not pre["ss"][0]:
            ahead_square(0)
        if not pre["norm"][0]:
            ahead_norm(0)
        xn32 = T_([D])
        xn32_b = Buf("xn32")
        xh = T_([D], BF16)
        xl = T_([D], BF16)
        xhl_b = Buf("xhl")
        xhT = T_([8, 128], BF16)
        xlT = T_([8, 128], BF16)
        xT_b = Buf("xhlT")
        wst = T_([8, 512])
        wst_b = Buf("wst")
        whi = T_([8, 512], BF16)
        wlo = T_([8, 512], BF16)
        whl_b = Buf("whl")
        qsb = T_([1024])
        qsb_b = Buf("qsb")
        prod = T_([512])
        prod_b = Buf("prod")
        gi = L * 7 + 2
        P.op("dve", lambda e: e.tensor_scalar(out=xn32, in0=x_sb[:, 0, :], scalar1=rstdP[a][:, 0:1], scalar2=None, op0=ALU.mult),
             reads=[xb[0], rstdP_b[a][0]], writes=[xn32_b])
        act(xh, xn32, AF.Copy, [xn32_b], [xhl_b])
        tt("dve", xl, xn32, xh, ALU.subtract, [xn32_b, xhl_b], [xhl_b])
        for (src, dst) in ((xh, xhT), (xl, xlT)):
            bk = A_next()
            tr_group([(psT[bk][:, k * 128:(k + 1) * 128], src[:, k * 128:(k + 1) * 128], identb) for k in range(8)],
                     [xhl_b, cst2_b], [psb[bk]])
            P.op("dve", lambda e, bk=bk, dst=dst: e.tensor_copy(out=dst, in_=psT[bk].rearrange("p (k t) -> p k t", k=8)),
                 reads=[psb[bk]], writes=[xT_b])
        wsrc = rqkvg_d[0]
        gb = gT[:, gi, :].unsqueeze(2).to_broadcast([128, 8, 512])
        for cb in range(4):
            dma(wst, wsrc[:, cb * 512:(cb + 1) * 512].rearrange("(k p) c -> p k c", p=128), "t0w", [], [wst_b], extra=last)
            tt("dve", wst, wst, gb, ALU.mult, [wst_b, cst2_b], [wst_b])
            act(whi.rearrange("p a b -> p (a b)"), wst.rearrange("p a b -> p (a b)"), AF.Copy, [wst_b], [whl_b])
            tt("dve", wlo, wst, whi, ALU.subtract, [wst_b, whl_b], [whl_b])
            bank = A_next()
            mms = []
            combos = [(xhT, whi), (xhT, wlo), (xlT, whi)]
            for ci, (xa, wa) in enumerate(combos):
                for k in range(8):
                    mms.append((ps[bank], xa[:, k, :], wa[:, k, :], ci == 0 and k == 0, ci == 2 and k == 7))
            mm_group(mms, [xT_b, whl_b], [psb[bank]])
            if cb < 2:
                act(qsb[:, cb * 512:(cb + 1) * 512], ps[bank], AF.Copy, [psb[bank]], [qsb_b])
            else:
                tt("dve", prod, ps[bank], qsb[:, (cb - 2) * 512:(cb - 1) * 512], ALU.mult, [psb[bank], qsb_b], [prod_b])
                for hh in range(2):
                    h = (cb - 2) * 2 + hh
                    act(junkP[:, 0:256], prod[:, hh * 256:(hh + 1) * 256], AF.Copy, [prod_b], [junkP_b, i00_b],
                        accum_out=i00P[:, h:h + 1])

    def ret(L, tile):
        last = new_phase("ret")
        gtab_load(L, 3, 1)
        cs = T_([2, T])
        cs_b = Buf("cs")
        dma(cs, rot_d[0:2, :, tile * T:(tile + 1) * T].rearrange("a p t -> p a t"), "rot0", [], [cs_b], extra=last)
        csq = [T_([2, T])]
        csq_b = [Buf("csq0")]
        xnT, xnT_b = prenorm()
        for kk in range(16):
            R_load(("mix", L, kk), kk)
        zT = T_([16, T], BF16)
        zT_b = [Buf("zTr%d" % c) for c in range(NCH)]
        ph["pn"] = (junkP, junkP_b, [T_([D])], [Buf("ptmp0")], [0])
        qdT = T_([2, T], BF16)
        qdT_b = Buf("qdT")
        kT = T_([2, T], BF16)
        kT_b = Buf("kTr")
        v_h = T_([NCH, 512], BF16)
        v_hb = Buf("v_h")
        sg_h = T_([NCH, 512], BF16)
        sg_hb = Buf("sg_h")
        tq = [T_([T]) for _ in range(2)]
        tq_b = [Buf("tq%d" % i) for i in range(2)]
        kd_sb = [T_([256], BF16) for _ in range(NCH)]
        kd_b = [Buf("kd%d" % i) for i in range(NCH)]
        inn = [T_([128], BF16) for _ in range(NCH)]
        inn_b = [Buf("inn%d" % i) for i in range(NCH)]
        stbf = [T_([2, 512], BF16) for _ in range(2)]
        stbf_b = [Buf("stbf0"), Buf("stbf1")]
        stats = [T_([6]) for _ in range(2)]
        stats_b = [Buf("stats0"), Buf("stats1")]
        mv = [T_([2]) for _ in range(2)]
        mv_b = [Buf("mv0"), Buf("mv1")]
        tno = [T_([512]) for _ in range(2)]
        tno_b = [Buf("tno0"), Buf("tno1")]
        z = [T_([512], BF16) for _ in range(2)]
        z_b = [Buf("z0"), Buf("z1")]
        for h in range(4):
            hs = 0
            dma(csq[hs], rot_d[2 + 2 * h:4 + 2 * h, :, tile * T:(tile + 1) * T].rearrange("a p t -> p a t"), "rotq%d" % hs, [], [csq_b[hs]],
                extra=last)

            def rotary(key, tab, tab_b, outT, outT_b):
                w, wb = ring_next(key)
                b0, b1 = A_next(), A_next()
                mm_group([(ps[b0], w[:, k, 0:128], xnT[:, k, :], k == 0, k == 7) for k in range(8)], [wb] + xnT_b, [psb[b0]])
                mm_group([(ps[b1], w[:, k, 128:256], xnT[:, k, :], k == 0, k == 7) for k in range(8)], [wb] + xnT_b, [psb[b1]])
                ring_done()
                tt("dve", tq[0], ps[b0], tab[:, 0, :], ALU.mult, [psb[b0], tab_b], [tq_b[0]])
                tt("dve", tq[1], ps[b1], tab[:, 1, :], ALU.mult, [psb[b1], tab_b], [tq_b[1]])
                tt("pool", outT[:, 0, :], tq[0], tq[1], ALU.subtract, [tq_b[0], tq_b[1]], [outT_b])
                tt("dve", tq[0], ps[b0], tab[:, 1, :], ALU.mult, [psb[b0], tab_b], [tq_b[0]])
                tt("dve", tq[1], ps[b1], tab[:, 0, :], ALU.mult, [psb[b1], tab_b], [tq_b[1]])
                tt("pool", outT[:, 1, :], tq[0], tq[1], ALU.add, [tq_b[0], tq_b[1]], [outT_b])

            rotary(("mix", L, 6 * h + 0), csq[hs], csq_b[hs], qdT, qdT_b)
            rotary(("mix", L, 6 * h + 1), cs, cs_b, kT, kT_b)
            for hv in range(2):
                w, wb = ring_next(("mix", L, 6 * h + 2 + hv))
                for c in range(NCH):
                    bk = A_next()
                    mm_group([(ps[bk][:, 0:256], xnT[:, k, c * 128:(c + 1) * 128], w[:, k, :], k == 0, k == 7) for k in range(8)],
                             [wb] + xnT_b, [psb[bk]])
                    act(v_h[:, c, hv * 256:(hv + 1) * 256], ps[bk][:, 0:256], AF.Copy, [psb[bk]], [v_hb])
                ring_done()
            for hg in range(2):
                w, wb = ring_next(("mix", L, 6 * h + 4 + hg))
                for c in range(NCH):
                    bk = A_next()
                    mm_group([(ps[bk][:, 0:256], xnT[:, k, c * 128:(c + 1) * 128], w[:, k, :], k == 0, k == 7) for k in range(8)],
                             [wb] + xnT_b, [psb[bk]])
                    act(sg_h[:, c, hg * 256:(hg + 1) * 256], ps[bk][:, 0:256], AF.Silu, [psb[bk]], [sg_hb])
                ring_done()
            for c in range(NCH):
                csl = slice(c * 128, (c + 1) * 128)
                bk = A_next()
                tr_group([(psT[bk][:, e * 128:(e + 1) * 128], kT[:, e, csl], identb) for e in range(2)], [kT_b, cst2_b], [psb[bk]])
                act(kd_sb[c], psT[bk][:, 0:256], AF.Copy, [psb[bk], cst_b], [kd_b[c]], scale=skd[:, h:h + 1])
                bi = A_next()
                mm_group([(ps[bi][:, 0:128], kT[:, e, csl], qdT[:, e, csl], e == 0, e == 1) for e in range(2)], [kT_b, qdT_b], [psb[bi]])
                dm_ = dmask0 if (tile == 0 and c == 0) else dmask
                tt("dve", inn[c], ps[bi][:, 0:128], dm_[:, h, :], ALU.mult, [psb[bi], cst_b], [inn_b[c]])
                if tile == 0 and c == 0:
                    stt(inn[0], e00, i00P[:, h:h + 1], inn[0], ALU.mult, ALU.add, [cst_b, i00_b, inn_b[0]], [inn_b[0]])

            def stage_a(c, h=h):
                gc = tile * NCH + c
                i2 = c % 2
                csl = slice(c * 128, (c + 1) * 128)
                if gc > 0:
                    act(stbf[i2].rearrange("p a b -> p (a b)"), state[:, h, :, :].rearrange("p a b -> p (a b)"), AF.Copy,
                        [state_b[h][0], state_b[h][1]], [stbf_b[i2]])
                yd = Y_next()
                mm_group([(ps[yd + e], kd_sb[c][:, e * 128:(e + 1) * 128], v_h[:, c, :], True, True) for e in range(2)],
                         [kd_b[c], v_hb], [psb[yd], psb[yd + 1]])
                for e in range(2):
                    stt(state[:, h, e, :], state[:, h, e, :], gC[h], ps[yd + e], ALU.mult, ALU.add,
                        [state_b[h][e], psb[yd + e]], [state_b[h][e]])
                bo = A_next()
                mms = [(ps[bo], inn[c], v_h[:, c, :], True, gc == 0)]
                rd = [inn_b[c], v_hb]
                if gc > 0:
                    for e in range(2):
                        mms.append((ps[bo], qdT[:, e, csl], stbf[i2][:, e, :], False, e == 1))
                    rd += [qdT_b, stbf_b[i2]]
                mm_group(mms, rd, [psb[bo]])
                P.op("dve", lambda e, bo=bo, i2=i2: e.bn_stats(out=stats[i2], in_=ps[bo]), reads=[psb[bo]], writes=[stats_b[i2]])
                P.op("dve", lambda e, i2=i2: e.bn_aggr(out=mv[i2], in_=stats[i2]), reads=[stats_b[i2]], writes=[mv_b[i2]])
                rs, rb = rstd_from(mv[i2][:, 1:2], mv_b[i2], 1.0, EPS)
                ts("dve", tno[i2], ps[bo], mv[i2][:, 0:1], rs[:, 0:1], ALU.subtract, ALU.mult, [psb[bo], mv_b[i2], rb], [tno_b[i2]])
                tt("pool", z[i2], tno[i2], sg_h[:, c, :], ALU.mult, [tno_b[i2], sg_hb], [z_b[i2]])

            def stage_b(c, h=h):
                i2 = c % 2
                csl = slice(c * 128, (c + 1) * 128)
                bz = A_next()
                tr_group([(psT[bz][:, i * 128:(i + 1) * 128], z[i2][:, i * 128:(i + 1) * 128], identb) for i in range(4)],
                         [z_b[i2], cst2_b], [psb[bz]])
                P.op("act", lambda e, bz=bz, h=h, csl=csl: e.activation(out=zT[:, h * 4:(h + 1) * 4, csl],
                                                                       in_=psT[bz][:, 0:512].rearrange("p (i t) -> p i t", i=4), func=AF.Copy),
                     reads=[psb[bz]], writes=[zT_b[c]])
                if h == 3:
                    out_proj(c, zT, [zT_b[c]], 16, False, gtab[1], gtab_b[1])

            stage_a(0)
            stage_a(1)
            stage_b(0)
            stage_a(2)
            stage_b(1)
            stage_a(3)
            stage_b(2)
            stage_b(3)
        phase_finish()

    for tile in range(n_tiles):
        pre_reset()
        if tile == 0:
            for c in range(NCH):
                dma(x_sb[:, c, :], x_d[c * 128:(c + 1) * 128, :], "xio%d" % c, [], [xb[c]])
        plan = []
        for L in layers:
            kind = L % 3
            if "f0" in subs:
                plan.append(("f0", L))
            if "mix" in subs:
                plan.append(("mix", L))
            if "f1" in subs:
                plan.append(("f1", L))
            if "ple" in subs:
                plan.append(("ple", L))
        if tile == 1 and bg["np"] < len(bgl):
            bg_require(len(bgl))
        for pi, (what, L) in enumerate(plan):
            kind = L % 3
            pre["ahead"] = pi + 1 < len(plan)
            pre["io"] = None if pre["ahead"] else tile
            if what == "f0":
                ffn(L, 0)
            elif what == "f1":
                ffn(L, 1)
            elif what == "ple":
                ple(L, tile)
            elif kind == 0:
                conv(L)
            elif kind == 1:
                swa(L, tile)
            else:
                if tile == 0:
                    ret_tok0(L)
                ret(L, tile)
    P.emit()
    return nc


_CONSTS = None


def _prep_inputs(inputs, b, consts, ntok=SEQ):
    f = lambda a: np.ascontiguousarray(np.asarray(a, dtype=np.float32))
    m = {
        "x": f(inputs["x"][b][:ntok]),
        "p": f(inputs["p"][:, b, :ntok]),
        "norm_g": f(inputs["norm_g"]).reshape(DEPTH * 7 * 8, 128),
        "ffn_w_gu": f(inputs["ffn_w_gu"]),
        "ffn_w_down": f(inputs["ffn_w_down"]),
        "ple_w_proj": f(inputs["ple_w_proj"]),
        "ple_w_gate": f(inputs["ple_w_gate"]),
        "conv_w_in": f(inputs["conv_w_in"]),
        "conv_w": f(inputs["conv_w"]).reshape(48, 128),
        "conv_w_out": f(inputs["conv_w_out"]),
        "swa_w_qkv": f(inputs["swa_w_qkv"]),
        "swa_sinks": f(inputs["swa_sinks"]),
        "swa_w_o": f(inputs["swa_w_o"]),
        "ret_w_qkvg": f(inputs["ret_w_qkvg"]),
        "ret_w_o": f(inputs["ret_w_o"]),
        "c_bias": consts["bias"],
        "c_rot": consts["rot"],
        "c_dmask": consts["dmask"].reshape(128, 512),
        "c_dmask0": consts["dmask0"].reshape(128, 512),
        "c_e00": consts["e00"],
        "c_skd": consts["skd"],
        "c_ident": consts["ident"],
    }
    return m


def kernel(**inputs):
    global _CONSTS
    if _CONSTS is None:
        _CONSTS = _host_consts()
    consts = dict(_CONSTS)
    consts["bias"] = _bias_table(np.asarray(inputs["rel_bias"], dtype=np.float32))
    nc = build(8, (0, 1, 2, 3), consts)
    in_maps = [_prep_inputs(inputs, b, consts) for b in range(8)]
    res = run_bass_kernel_spmd(nc, in_maps, core_ids=list(range(8)))
    out = np.stack([np.asarray(r["out"], dtype=np.float32) for r in res.results], axis=0)
    return out
```

```python
import math
import numpy as np
import concourse.bass as bass
import concourse.mybir as mybir
from concourse.bass_utils import run_bass_kernel_spmd

F32 = mybir.dt.float32
BF16 = mybir.dt.bfloat16
AF = mybir.ActivationFunctionType
ALU = mybir.AluOpType

D = 1024
T = 512
NCH = 4
DFF = 2816
NJ = 22
SEQ = 4096
DEPTH = 4
EPS = 1e-6
NRING = 5
NR = 22
NEG = -30000.0


class Buf:
    __slots__ = ("name", "w", "r", "excl")

    def __init__(self, name, excl=False):
        self.name = name
        self.w = None
        self.r = []
        self.excl = excl


class Op:
    __slots__ = ("eng", "fn", "deps", "sig", "cnt", "dsem", "dcnt")


COMPUTE = ("pe", "act", "dve", "pool")
ENGS = ("pe", "act", "dve", "pool", "sp")


class Prog:
    def __init__(self, nc):
        self.nc = nc
        self.ops = {e: [] for e in ENGS}
        self.dma_counts = {}
        self.bar = {e: [] for e in ENGS}

    def op(self, eng, fn, reads=(), writes=(), dsem=None, extra=()):
        o = Op()
        o.eng = eng
        o.fn = fn
        o.sig = False
        o.dsem = dsem
        o.cnt = 0
        o.dcnt = 0
        deps = {}
        is_dma = dsem is not None

        def add(d, raw):
            if d is None:
                return
            if (not is_dma) and d.dsem is None and d.eng == eng:
                if eng == "pe":
                    return
            deps[id(d)] = d

        for b in reads:
            add(b.w, True)
            if b.excl:
                for r in b.r:
                    add(r, False)
        for b in writes:
            add(b.w, False)
            for r in b.r:
                add(r, False)
        for d in extra:
            if d is not None:
                deps[id(d)] = d
        for d in self.bar[eng]:
            deps[id(d)] = d
        self.bar[eng] = []
        for b in writes:
            b.w = o
            b.r = []
        for b in reads:
            if b.w is o:
                continue
            if not is_dma:
                b.r = [r for r in b.r if not (r.eng == eng and r.dsem is None)]
            b.r.append(o)
        for d in deps.values():
            if d.dsem is None:
                d.sig = True
        o.deps = list(deps.values())
        if is_dma:
            c = self.dma_counts.get(dsem, 0) + 1
            self.dma_counts[dsem] = c
            o.dcnt = c
        self.ops[eng].append(o)
        return o

    def barrier(self):
        last = [self.ops[e][-1] for e in COMPUTE if self.ops[e]]
        for e in COMPUTE:
            for l in last:
                if l.eng != e:
                    self.bar[e].append(l)
        return last

    def emit(self):
        nc = self.nc
        esem = {e: nc.alloc_semaphore("es_" + e) for e in COMPUTE}
        dsem = {k: nc.alloc_semaphore("ds_" + k) for k in self.dma_counts}
        for e in COMPUTE:
            c = 0
            for o in self.ops[e]:
                if o.sig:
                    c += 1
                    o.cnt = c
        engobj = {"pe": "tensor", "act": "scalar", "dve": "vector", "pool": "gpsimd", "sp": "sync"}
        final = dict(self.dma_counts)

        def run(ename, eng):
            waited = {}
            for o in self.ops[ename]:
                need = {}
                for d in o.deps:
                    if d.dsem is not None:
                        key = ("d", d.dsem)
                        val = d.dcnt * 16
                    else:
                        key = ("e", d.eng)
                        val = d.cnt
                    if need.get(key, 0) < val:
                        need[key] = val
                for key, val in need.items():
                    if waited.get(key, 0) < val:
                        s = dsem[key[1]] if key[0] == "d" else esem[key[1]]
                        eng.wait_ge(s, val)
                        waited[key] = val
                ins = o.fn(eng)
                if o.dsem is not None:
                    ins.then_inc(dsem[o.dsem], 16)
                elif o.sig:
                    ins.then_inc(esem[ename], 1)
            if ename == "sp":
                for k, c in final.items():
                    if waited.get(("d", k), 0) < c * 16:
                        eng.wait_ge(dsem[k], c * 16)

        with nc.Block() as block:
            @block.sync
            def _(e):
                run("sp", e)

            @block.tensor
            def _(e):
                run("pe", e)

            @block.scalar
            def _(e):
                run("act", e)

            @block.vector
            def _(e):
                run("dve", e)

            @block.gpsimd
            def _(e):
                run("pool", e)


def _rel_bucket_np(dist):
    max_exact = 16
    d = np.maximum(dist, 1).astype(np.float32)
    large = max_exact + (np.log(d / max_exact) / math.log(128 / max_exact) * (32 - max_exact)).astype(np.int32)
    large = np.minimum(large, 31)
    return np.where(dist < max_exact, dist, large)


def _host_consts():
    c = {}
    c["ident"] = np.eye(128, dtype=np.float32)
    H, dk, C = 4, 256, 128
    half = dk // 2
    inv = (10000.0 ** (-np.arange(half, dtype=np.float32) / half)).astype(np.float32)
    pos = np.arange(SEQ, dtype=np.float32)
    ang = (pos[None, :] * inv[:, None]).astype(np.float32)
    cos = np.cos(ang).astype(np.float32)
    sin = np.sin(ang).astype(np.float32)
    gam = (1.0 - 2.0 ** (-5.0 - np.arange(H, dtype=np.float64)))
    j = (np.arange(SEQ) % C).astype(np.float64)
    rot = np.zeros((10, 128, SEQ), np.float32)
    rot[0] = cos
    rot[1] = sin
    for h in range(H):
        dq = gam[h] ** (j + 1.0)
        rot[2 + 2 * h] = (cos * dq[None, :]).astype(np.float32)
        rot[3 + 2 * h] = (sin * dq[None, :]).astype(np.float32)
    c["rot"] = rot
    kt = np.arange(C, dtype=np.float64)
    dm = np.zeros((128, H, 128), np.float32)
    skd = np.zeros((128, H), np.float32)
    for h in range(H):
        m = (gam[h] ** (-(kt + 1.0)) / 16.0)[:, None] * (kt[None, :] >= kt[:, None])
        dm[:, h, :] = m.astype(np.float32)
        skd[:, h] = (gam[h] ** (C - 1.0 - kt) / 16.0).astype(np.float32)
    c["dmask"] = dm
    dm0 = dm.copy()
    dm0[0, :, 0] = 0.0
    c["dmask0"] = dm0
    e00 = np.zeros((128, 128), np.float32)
    e00[0, 0] = 1.0 / 16.0
    c["e00"] = e00
    c["skd"] = skd
    c["gC"] = [float(gam[h] ** C) for h in range(H)]
    return c


def _bias_table(rel_bias):
    qi = np.arange(128)[None, :]
    out = np.empty((128, 2, 2, 8, 128), np.float32)
    for kb in range(2):
        kk = (np.arange(128) + kb * 128)[:, None]
        dist = qi + 128 - kk
        valid = (dist >= 0) & (dist < 128)
        bidx = _rel_bucket_np(np.maximum(dist, 0))
        gathered = rel_bias[bidx]
        for kv in range(2):
            for g in range(8):
                out[:, kv, kb, g, :] = np.where(valid, gathered[:, :, kv * 8 + g], np.float32(NEG))
    return out.reshape(128, 4096)


def build(n_tiles=8, layers=(0, 1, 2, 3), consts=None, subs=("f0", "mix", "f1", "ple")):
    nc = bass.Bass("TRN2", target_bir_lowering=False)
    P = Prog(nc)
    gC = consts["gC"]
    ntok = n_tiles * T

    def din(name, shape, dt=F32):
        return nc.dram_tensor(name, list(shape), dt, kind="ExternalInput").ap()

    x_d = din("x", [ntok, D])
    p_d = din("p", [DEPTH, ntok, 256])
    normg_d = din("norm_g", [DEPTH * 7 * 8, 128])
    wgu_d = din("ffn_w_gu", [DEPTH, 2, D, 2 * DFF])
    wdn_d = din("ffn_w_down", [DEPTH, 2, DFF, D])
    wpp_d = din("ple_w_proj", [DEPTH, 256, D])
    wpg_d = din("ple_w_gate", [DEPTH, D, D])
    cwin_d = din("conv_w_in", [2, D, 3 * D])
    cw_d = din("conv_w", [48, 128])
    cwout_d = din("conv_w_out", [2, D, D])
    sqkv_d = din("swa_w_qkv", [1, D, 1280])
    sink_d = din("swa_sinks", [1, 16])
    swo_d = din("swa_w_o", [1, D, D])
    rqkvg_d = din("ret_w_qkvg", [1, D, 6144])
    rwo_d = din("ret_w_o", [1, 2048, D])
    bm_d = din("c_bias", [128, 4096])
    rot_d = din("c_rot", [10, 128, SEQ])
    dmask_d = din("c_dmask", [128, 512])
    dmask0_d = din("c_dmask0", [128, 512])
    e00_d = din("c_e00", [128, 128])
    skd_d = din("c_skd", [128, 4])
    ident_d = din("c_ident", [128, 128])
    out_d = nc.dram_tensor("out", [ntok, D], F32, kind="ExternalOutput").ap()

    ring_items = {}
    R_items = {}
    ring_list = []
    R_list = []

    ring_groups = []

    def add_group(loads, stage_cols, items):
        its = []
        for (key, gidx, pieces) in items:
            ring_items[key] = len(ring_list)
            its.append((len(ring_list), gidx, pieces))
            ring_list.append(key)
        ring_groups.append((loads, stage_cols, its, items[0][0][1]))

    R_direct = {}

    def add_R(key, srcs, fold=None):
        if fold is None and key[0] != "ple":
            R_direct[key] = srcs
            return
        R_items[key] = len(R_list)
        R_list.append((srcs, fold, key[1]))

    for L in layers:
        kind, jj = L % 3, L // 3
        for f in range(2):
            w = wgu_d[L, f]
            gi = L * 7 + (0 if f == 0 else 4)
            for (j0, j1) in ((0, 8), (8, 16), (16, 22)):
                n = j1 - j0
                add_group([(0, w, j0 * 128, n * 128), (n * 128, w, DFF + j0 * 128, n * 128)], 2 * n * 128,
                          [(("ffn", L, f, j), gi, [(0, (j - j0) * 128, 128), (128, n * 128 + (j - j0) * 128, 128)]) for j in range(j0, j1)])
            for j in range(NJ):
                add_R(("ffn", L, f, j), [(wdn_d[L, f], j * 128, 128, 0)])
        if kind == 0:
            w = cwin_d[jj]
            for m0 in (0, 4):
                nm = 4
                loads = [(t * nm * 128, w, t * 1024 + m0 * 128, nm * 128) for t in range(3)]
                items = []
                for i in range(m0 * 3 // 2, (m0 + nm) * 3 // 2):
                    pieces = []
                    for s_ in (2 * i, 2 * i + 1):
                        m, t = s_ // 3, s_ % 3
                        pieces.append(((s_ % 2) * 128, t * nm * 128 + (m - m0) * 128, 128))
                    items.append((("mix", L, i), L * 7 + 2, pieces))
                add_group(loads, 3 * nm * 128, items)
            for m in range(8):
                add_R(("mix", L, m), [(cwout_d[jj], m * 128, 128, 0)])
        elif kind == 1:
            w = sqkv_d[0]
            items = []
            for i in range(4):
                pieces = []
                for hf, c in enumerate((2 * i, 2 * i + 1)):
                    pieces.append((hf * 128, c * 64, 64))
                    pieces.append((hf * 128 + 64, (8 + c) * 64, 64))
                items.append((("mix", L, i), L * 7 + 2, pieces))
            items.append((("mix", L, 4), L * 7 + 2, [(0, 1024, 256)]))
            add_group([(0, w, 0, 1280)], 1280, items)
            for c in range(8):
                add_R(("mix", L, c), [(swo_d[0], c * 64, 64, 0), (swo_d[0], (8 + c) * 64, 64, 64)])
        else:
            w = rqkvg_d[0]
            gi = L * 7 + 2
            add_group([(0, w, 0, 1024)], 1024, [(("mix", L, 6 * h + 0), gi, [(0, h * 256, 256)]) for h in range(4)])
            add_group([(0, w, 1024, 1024)], 1024, [(("mix", L, 6 * h + 1), gi, [(0, h * 256, 256)]) for h in range(4)])
            for base, t0 in ((2048, 2), (4096, 4)):
                for hp in range(2):
                    items = []
                    for h in (2 * hp, 2 * hp + 1):
                        for hv in range(2):
                            items.append((("mix", L, 6 * h + t0 + hv), gi, [(0, (h - 2 * hp) * 512 + hv * 256, 256)]))
                    add_group([(0, w, base + hp * 1024, 1024)], 1024, items)
            for kk in range(16):
                add_R(("mix", L, kk), [(rwo_d[0], kk * 128, 128, 0)])
        for k in range(8):
            add_R(("ple", L, k), [(wpg_d[L], k * 128, 128, 0)], fold=(L * 7 + 6, k))
        for k in range(2):
            add_R(("ple", L, 8 + k), [(wpp_d[L], k * 128, 128, 0)])

    wsA = nc.dram_tensor("wsA", [len(ring_list), 128, 2048], BF16, kind="Internal").ap()
    wsR = nc.dram_tensor("wsR", [len(R_list), 128, 1024], BF16, kind="Internal").ap()
    wsA_b = [Buf("wsA%d" % i) for i in range(len(ring_list))]
    wsR_b = [Buf("wsR%d" % i) for i in range(len(R_list))]

    SLAB_COLS = 52992
    slab_t = nc.alloc_sbuf_tensor("slab", [128, SLAB_COLS], F32)
    slab = slab_t.ap() if hasattr(slab_t, "ap") else slab_t
    ps_t = nc.alloc_psum_tensor("ps", [128, 8, 512], F32)
    ps_all = ps_t.ap() if hasattr(ps_t, "ap") else ps_t
    cur = [0]

    def alloc(nbytes):
        o = cur[0]
        cur[0] = o + ((nbytes + 63) // 64) * 64
        assert cur[0] <= SLAB_COLS * 4, "SBUF overflow %d" % cur[0]
        return o

    def view(off, shape, dt=F32):
        n = 1
        for s in shape:
            n *= s
        sz = 4 if dt == F32 else 2
        ap = slab[:, off // 4: (off + n * sz) // 4]
        if dt != F32:
            ap = ap.bitcast(dt)
        if len(shape) == 2:
            ap = ap.rearrange("p (a b) -> p a b", a=shape[0])
        elif len(shape) == 3:
            ap = ap.rearrange("p (a b c) -> p a b c", a=shape[0], b=shape[1])
        return ap

    def T_(shape, dt=F32):
        n = 1
        for s in shape:
            n *= s
        return view(alloc(n * (4 if dt == F32 else 2)), shape, dt)

    identf = T_([128])
    identb = T_([128], BF16)
    onesb = T_([128], BF16)
    gT = T_([28, 8])
    cwT = T_([2, 3, 8])
    expsink = T_([16])
    dmask = T_([4, 128])
    dmask0 = T_([4, 128])
    e00 = T_([128])
    skd = T_([4])
    mhalf = T_([4])
    sA = alloc(0)
    stg0 = T_([128])
    stg1 = T_([128])
    stg2 = T_([128])
    PRE0 = cur[0]
    x_sb = T_([NCH, D])
    xb = [Buf("x%d" % c) for c in range(NCH)]
    ring = [T_([8, 256], BF16) for _ in range(NRING)]
    ring_b = [Buf("ring%d" % i) for i in range(NRING)]
    Rs = [T_([1024], BF16) for _ in range(NR)]
    R_b = [Buf("R%d" % i) for i in range(NR)]
    gtab = [T_([D]) for _ in range(3)]
    gtab_b = [Buf("gtab%d" % i) for i in range(3)]
    state = T_([4, 2, 512])
    state_b = [[Buf("st%d%d" % (h, e)) for e in range(2)] for h in range(4)]
    ucarry = [T_([8, 2]) for _ in range(2)]
    ucarry_b = [Buf("ucar%d" % i) for i in range(2)]
    kcarry = T_([128], BF16)
    vcarry = T_([128], BF16)
    kvcarry_b = Buf("kvcar")
    cst_b = Buf("consts")
    junkP = T_([D], BF16)
    junkP_b = Buf("junkP")
    ssP = [T_([NCH]) for _ in range(2)]
    ssP_b = [[Buf("ssP%d%d" % (a, c)) for c in range(NCH)] for a in range(2)]
    xnTP = [T_([8, T], BF16) for _ in range(2)]
    xnTP_b = [[Buf("xnT%d%d" % (a, c)) for c in range(NCH)] for a in range(2)]
    xnbP = [T_([D], BF16) for _ in range(NCH)]
    xnbP_b = [Buf("xnb%d" % c) for c in range(NCH)]
    varP = [T_([NCH]) for _ in range(2)]
    varP_b = [[Buf("varP%d%d" % (a, c)) for c in range(NCH)] for a in range(2)]
    rstdP = [T_([NCH]) for _ in range(2)]
    rstdP_b = [[Buf("rstdP%d%d" % (a, c)) for c in range(NCH)] for a in range(2)]
    i00P = T_([4])
    i00_b = Buf("i00")
    pre = {"par": 0, "ss": [False] * NCH, "norm": [False] * NCH, "T": [False] * NCH, "pendT": None, "ahead": True}

    def pre_reset():
        pre["ss"] = [False] * NCH
        pre["norm"] = [False] * NCH
        pre["T"] = [False] * NCH
        pre["pendT"] = None
    ARENA0 = cur[0]

    ps = [ps_all[:, i, :] for i in range(8)]
    psb = [Buf("ps%d" % i, excl=True) for i in range(8)]
    psT = [ps_all[:, i, :].bitcast(BF16) for i in range(8)]
    a_ctr = [0]
    y_ctr = [0]

    def A_next():
        i = a_ctr[0] % 8
        a_ctr[0] += 1
        return i

    def Y_next():
        i = 2 * (y_ctr[0] % 4)
        y_ctr[0] += 1
        return i

    def ps2(i):
        return ps_all[:, i:i + 2, :].rearrange("p a b -> p (a b)")

    def dma(out, in_, dsem, reads, writes, extra=()):
        return P.op("sp", lambda e: e.dma_start(out=out, in_=in_), reads=reads, writes=writes, dsem=dsem, extra=extra)

    def act(out, in_, func, reads, writes, scale=1.0, accum_out=None):
        def fn(e):
            kw = {}
            if accum_out is not None:
                kw["accum_out"] = accum_out
            return e.activation(out=out, in_=in_, func=func, scale=scale, **kw)
        return P.op("act", fn, reads=reads, writes=writes)

    def tt(eng, out, in0, in1, op, reads, writes):
        return P.op(eng, lambda e: e.tensor_tensor(out=out, in0=in0, in1=in1, op=op), reads=reads, writes=writes)

    def stt(out, in0, scalar, in1, op0, op1, reads, writes):
        return P.op("dve", lambda e: e.scalar_tensor_tensor(out=out, in0=in0, scalar=scalar, in1=in1, op0=op0, op1=op1),
                    reads=reads, writes=writes)

    def ts(eng, out, in0, s1, s2, op0, op1, reads, writes):
        return P.op(eng, lambda e: e.tensor_scalar(out=out, in0=in0, scalar1=s1, scalar2=s2, op0=op0, op1=op1),
                    reads=reads, writes=writes)

    def mm_group(mms, reads, writes):
        def fn(e):
            ins = None
            for (o, l, r, st, sp) in mms:
                ins = e.matmul(o, lhsT=l, rhs=r, start=st, stop=sp)
            return ins
        return P.op("pe", fn, reads=reads, writes=writes)

    def tr_group(trs, reads, writes):
        def fn(e):
            ins = None
            for (o, i_, idn) in trs:
                ins = e.transpose(o, i_, idn)
            return ins
        return P.op("pe", fn, reads=reads, writes=writes)

    dma(identf, ident_d, "cst", [], [cst_b])
    dma(dmask.rearrange("p a b -> p (a b)"), dmask_d, "cst", [], [cst_b])
    dma(dmask0.rearrange("p a b -> p (a b)"), dmask0_d, "cst", [], [cst_b])
    dma(e00, e00_d, "cst", [], [cst_b])
    dma(skd, skd_d, "cst", [], [cst_b])
    dma(expsink, sink_d.partition_broadcast(128), "cst", [], [cst_b])
    stg_b = Buf("stg")
    dma(stg0[0:112, :], normg_d[0:112, :], "cst2", [], [stg_b])
    dma(stg1[0:112, :], normg_d[112:224, :], "cst2", [], [stg_b])
    dma(stg2[0:48, :], cw_d, "cst2", [], [stg_b])
    cst2_b = Buf("consts2")
    act(identb, identf, AF.Copy, [cst_b], [cst2_b])
    act(expsink, expsink, AF.Exp, [cst_b], [cst2_b])
    P.op("dve", lambda e: e.memset(onesb, 1.0), writes=[cst2_b])
    P.op("dve", lambda e: e.memset(mhalf, -0.5), writes=[cst2_b])
    tr_group([(ps[0][:, 0:112], stg0[0:112, :], identf[0:112, 0:112]),
              (ps[0][:, 112:224], stg1[0:112, :], identf[0:112, 0:112]),
              (ps[0][:, 256:304], stg2[0:48, :], identf[0:48, 0:48])], [stg_b, cst_b], [psb[0]])
    P.op("dve", lambda e: e.tensor_copy(out=gT.rearrange("p a b -> p (a b)"), in_=ps[0][:, 0:224]), reads=[psb[0]], writes=[cst2_b])
    P.op("dve", lambda e: e.tensor_copy(out=cwT.rearrange("p a b c -> p (a b c)"), in_=ps[0][:, 256:304]), reads=[psb[0]], writes=[cst2_b])

    SETB = 64 * 1024
    st_off = [PRE0, PRE0 + SETB]
    ob_off = PRE0 + 2 * SETB
    NOB = 4
    obuf = [view(ob_off + i * 4096, [8, 256], BF16) for i in range(NOB)]
    obuf_b = [Buf("obuf%d" % i) for i in range(NOB)]
    stset_b = [Buf("stset0"), Buf("stset1")]
    assert ob_off + NOB * 4096 <= SLAB_COLS * 4
    obc = [0]

    FG = set(layers[:2]) if len(layers) > 2 else set(layers)
    BGL = [L for L in layers if L not in FG]
    fg_groups = [g for g in ring_groups if g[3] in FG]
    fg_R = [n for n in range(len(R_list)) if R_list[n][2] in FG]
    for gi_, (loads, stage_cols, its, _lay) in enumerate(fg_groups):
        sset = gi_ % 2
        stg = view(st_off[sset], [8, stage_cols])
        for (dcol, w, c0, ncol) in loads:
            dma(stg[:, :, dcol:dcol + ncol], w[:, c0:c0 + ncol].rearrange("(k p) c -> p k c", p=128),
                "stg%d" % sset, [], [stset_b[sset]])
        for (ridx, gidx, pieces) in its:
            o = obc[0] % NOB
            obc[0] += 1
            eng = "dve" if (obc[0] % 2 == 0) else "pool"
            for (ocol, scol, n) in pieces:
                gb = gT[:, gidx, :].unsqueeze(2).to_broadcast([128, 8, n])
                tt(eng, obuf[o][:, :, ocol:ocol + n], stg[:, :, scol:scol + n], gb, ALU.mult, [stset_b[sset], cst2_b], [obuf_b[o]])
            dma(wsA[ridx], obuf[o].rearrange("p a b -> p (a b)"), "ob%d" % o, [obuf_b[o]], [wsA_b[ridx]])

    NST = 4
    stage = [view(st_off[0] + i * 4096, [1024]) for i in range(NST)]
    stage_b = [Buf("stage%d" % i) for i in range(NST)]
    fence_r = list(stset_b[0].r) + ([stset_b[0].w] if stset_b[0].w is not None else [])

    def pre_load(m_):
        n = fg_R[m_]
        srcs, fold, _lay = R_list[n]
        s_ = m_ % NST
        for (w, r0, nr, p0) in srcs:
            dma(stage[s_][p0:p0 + nr, :], w[r0:r0 + nr, :], "rstg%d" % s_, [], [stage_b[s_]], extra=fence_r if m_ < NST else ())

    def pre_proc(m_):
        n = fg_R[m_]
        srcs, fold, _lay = R_list[n]
        s_ = m_ % NST
        o = obc[0] % NOB
        obc[0] += 1
        ob2 = obuf[o].rearrange("p a b -> p (a b)")
        sc = 1.0 if fold is None else gT[:, fold[0], fold[1]:fold[1] + 1]
        act(ob2[:, 0:1024], stage[s_], AF.Copy, [stage_b[s_], cst2_b], [obuf_b[o]], scale=sc)
        dma(wsR[n], ob2[:, 0:1024], "ob%d" % o, [obuf_b[o]], [wsR_b[n]])

    nit = len(fg_R)
    for n in range(nit + 3):
        if n < nit:
            pre_load(n)
        if n >= 3:
            pre_proc(n - 3)
    last_pre = P.barrier()
    fence = []
    for b_ in obuf_b + stage_b + stset_b:
        for r_ in b_.r:
            if r_.dsem is not None:
                fence.append(r_)
        if b_.w is not None and b_.w.dsem is not None:
            fence.append(b_.w)
    for e_ in COMPUTE:
        P.bar[e_] += fence
    P.bar["sp"] += fence + list(last_pre)
    cur[0] = ARENA0
    P.op("dve", lambda e: e.memset(state.rearrange("p a b c -> p (a b c)"), 0.0), writes=[b for hb in state_b for b in hb])
    P.op("dve", lambda e: e.memset(ucarry[0].rearrange("p a b -> p (a b)"), 0.0), writes=[ucarry_b[0]])
    P.op("dve", lambda e: e.memset(ucarry[1].rearrange("p a b -> p (a b)"), 0.0), writes=[ucarry_b[1]])
    P.op("dve", lambda e: e.memset(kcarry, 0.0), writes=[kvcarry_b])
    P.op("dve", lambda e: e.memset(vcarry, 0.0), writes=[kvcarry_b])

    TOP = SLAB_COLS * 4
    BG_BYTES = 24 * 1024
    bgs = [view(TOP - BG_BYTES + i * 8192, [8, 256]) for i in range(2)]
    bgo = [view(TOP - 8192 + i * 4096, [8, 256], BF16) for i in range(2)]
    bgs_b = [Buf("bgs0"), Buf("bgs1")]
    bgo_b = [Buf("bgo0"), Buf("bgo1")]
    bgl = []
    bg_end = {}
    for L in BGL:
        for (loads, stage_cols, its, lay) in ring_groups:
            if lay != L:
                continue
            for (ridx, gidx, pieces) in its:
                pcs = []
                for (ocol, scol, n) in pieces:
                    for (dcol, w, c0, ncol) in loads:
                        if dcol <= scol < dcol + ncol:
                            pcs.append((ocol, w, c0 + (scol - dcol), n))
                            break
                assert len(pcs) == len(pieces)
                bgl.append(("A", ridx, gidx, pcs))
        for n in range(len(R_list)):
            if R_list[n][2] == L:
                bgl.append(("R", n))
        bg_end[L] = len(bgl)
    bg = {"nl": 0, "np": 0, "extra": (), "on": False, "stores": []}

    def bg_load(n):
        it = bgl[n]
        s_ = n % 2
        if it[0] == "A":
            for (ocol, w, c0, ncol) in it[3]:
                dma(bgs[s_][:, :, ocol:ocol + ncol], w[:, c0:c0 + ncol].rearrange("(k p) c -> p k c", p=128),
                    "bgs%d" % s_, [], [bgs_b[s_]], extra=bg["extra"])
        else:
            srcs, fold, _lay = R_list[it[1]]
            st2 = bgs[s_].rearrange("p a b -> p (a b)")
            for (w, r0, nr, p0) in srcs:
                dma(st2[p0:p0 + nr, 0:1024], w[r0:r0 + nr, :], "bgs%d" % s_, [], [bgs_b[s_]], extra=bg["extra"])

    def bg_proc(n):
        it = bgl[n]
        s_ = n % 2
        if it[0] == "A":
            _, ridx, gidx, pcs = it
            gb = gT[:, gidx, :].unsqueeze(2).to_broadcast([128, 8, 256])
            tt("pool", bgo[s_], bgs[s_], gb, ALU.mult, [bgs_b[s_], cst2_b], [bgo_b[s_]])
            st = dma(wsA[ridx], bgo[s_].rearrange("p a b -> p (a b)"), "bgo%d" % s_, [bgo_b[s_]], [wsA_b[ridx]])
        else:
            nR = it[1]
            srcs, fold, _lay = R_list[nR]
            st2 = bgs[s_].rearrange("p a b -> p (a b)")
            ob2 = bgo[s_].rearrange("p a b -> p (a b)")
            sc = 1.0 if fold is None else gT[:, fold[0], fold[1]:fold[1] + 1]
            act(ob2[:, 0:1024], st2[:, 0:1024], AF.Copy, [bgs_b[s_], cst2_b], [bgo_b[s_]], scale=sc)
            st = dma(wsR[nR], ob2[:, 0:1024], "bgo%d" % s_, [bgo_b[s_]], [wsR_b[nR]])
        bg["stores"] = (bg["stores"] + [st])[-2:]
        bg["fenced"] = False

    def bg_pump(k=1):
        if not bg["on"]:
            return
        for _ in range(k):
            if bg["np"] < bg["nl"] and (bg["nl"] - bg["np"] == 2 or bg["nl"] == len(bgl)):
                bg_proc(bg["np"])
                bg["np"] += 1
            if bg["nl"] < len(bgl) and bg["nl"] - bg["np"] < 2:
                bg_load(bg["nl"])
                bg["nl"] += 1

    def bg_drain():
        while bg["np"] < bg["nl"]:
            bg_proc(bg["np"])
            bg["np"] += 1

    def bg_require(upto):
        assert not bg.get("paused"), "background conversion deadline inside a phase whose arena overlaps its staging"
        while bg["np"] < upto:
            if bg["nl"] <= bg["np"]:
                bg_load(bg["nl"])
                bg["nl"] += 1
            bg_proc(bg["np"])
            bg["np"] += 1

    ring_seq = []
    for t in range(n_tiles):
        for L in layers:
            kind = L % 3
            nm = (12, 5, 24)[kind]
            if "f0" in subs:
                ring_seq += [("ffn", L, 0, j) for j in range(NJ)]
            if "mix" in subs:
                ring_seq += [("mix", L, i) for i in range(nm)]
            if "f1" in subs:
                ring_seq += [("ffn", L, 1, j) for j in range(NJ)]
    rstate = {"issued": 0, "consumed": 0}

    def ring_prefetch():
        while rstate["issued"] < len(ring_seq) and rstate["issued"] < rstate["consumed"] + NRING:
            n = rstate["issued"]
            s = n % NRING
            Lk = ring_seq[n][1]
            if Lk in bg_end and bg["np"] < bg_end[Lk]:
                bg_require(bg_end[Lk])
            i = ring_items[ring_seq[n]]
            dma(ring[s].rearrange("p a b -> p (a b)"), wsA[i], "ring%d" % s, [wsA_b[i]], [ring_b[s]])
            rstate["issued"] += 1

    def ring_next(key):
        n = rstate["consumed"]
        assert ring_seq[n] == key, (ring_seq[n], key)
        if rstate["issued"] <= n:
            ring_prefetch()
        s = n % NRING
        return ring[s], ring_b[s]

    def ring_done():
        rstate["consumed"] += 1
        ring_prefetch()

    def R_load(key, slot):
        if key in R_direct:
            for (w, r0, nr, p0) in R_direct[key]:
                P.op("pool", lambda e, w=w, r0=r0, nr=nr, p0=p0: e.dma_start(out=Rs[slot][p0:p0 + nr, :], in_=w[r0:r0 + nr, :]),
                     reads=[], writes=[R_b[slot]], dsem="Rg%d" % slot)
            return
        i = R_items[key]
        dma(Rs[slot], wsR[i], "R%d" % slot, [wsR_b[i]], [R_b[slot]])

    def rstd_from(ss, ssb_, scale, eps, n=1):
        var = T_([n])
        rs = T_([n])
        vb, rb = Buf("var"), Buf("rstd")
        ts("dve", var, ss, scale, eps, ALU.mult, ALU.add, [ssb_], [vb])
        tt("pool", rs, var, mhalf[:, 0:n], ALU.pow, [vb, cst2_b], [rb])
        return rs, rb

    ph = {}

    def new_phase(kind="x"):
        if bg["on"]:
            assert cur[0] <= TOP - BG_BYTES, "arena overlaps background staging: %d" % cur[0]
        pause = kind in ("swa", "ret", "tok0")
        if pause:
            bg_drain()
        last = P.barrier()
        if pause and not bg.get("fenced", True):
            for e_ in COMPUTE:
                P.bar[e_] += list(bg["stores"])
            bg["fenced"] = True
        bg["on"] = (bg["np"] < len(bgl)) and not pause
        bg["paused"] = pause
        bg["extra"] = tuple(last)
        cur[0] = ARENA0
        ph.clear()
        return last

    def ahead_square(c):
        a = pre["par"]
        act(junkP, x_sb[:, c, :], AF.Square, [xb[c]], [junkP_b, ssP_b[a][c]], accum_out=ssP[a][:, c:c + 1])
        pre["ss"][c] = True

    def ahead_norm(c):
        a = pre["par"]
        ts("dve", varP[a][:, c:c + 1], ssP[a][:, c:c + 1], 1.0 / D, EPS, ALU.mult, ALU.add, [ssP_b[a][c]], [varP_b[a][c]])
        tt("pool", rstdP[a][:, c:c + 1], varP[a][:, c:c + 1], mhalf[:, 0:1], ALU.pow, [varP_b[a][c], cst2_b], [rstdP_b[a][c]])
        if c % 2 == 0:
            act(xnbP[c], x_sb[:, c, :], AF.Copy, [xb[c], rstdP_b[a][c]], [xnbP_b[c]], scale=rstdP[a][:, c:c + 1])
        else:
            P.op("dve", lambda e, c=c, a=a: e.tensor_scalar(out=xnbP[c], in0=x_sb[:, c, :], scalar1=rstdP[a][:, c:c + 1], scalar2=None,
                                                           op0=ALU.mult), reads=[xb[c], rstdP_b[a][c]], writes=[xnbP_b[c]])
        pre["norm"][c] = True

    def emit_T(c):
        a = pre["par"]
        bk = A_next()
        tr_group([(psT[bk][:, k * 128:(k + 1) * 128], xnbP[c][:, k * 128:(k + 1) * 128], identb) for k in range(8)],
                 [xnbP_b[c], cst2_b], [psb[bk]])
        if c % 2 == 0:
            P.op("dve", lambda e, bk=bk, c=c, a=a: e.tensor_copy(out=xnTP[a][:, :, c * 128:(c + 1) * 128],
                                                                in_=psT[bk].rearrange("p (k t) -> p k t", k=8)),
                 reads=[psb[bk]], writes=[xnTP_b[a][c]])
        else:
            P.op("act", lambda e, bk=bk, c=c, a=a: e.activation(out=xnTP[a][:, :, c * 128:(c + 1) * 128],
                                                               in_=psT[bk].rearrange("p (k t) -> p k t", k=8), func=AF.Copy),
                 reads=[psb[bk]], writes=[xnTP_b[a][c]])
        pre["T"][c] = True

    def chunk_final(c):
        if not pre["ahead"]:
            if pre.get("io") is not None:
                tl = pre["io"]
                dma(out_d[tl * T + c * 128: tl * T + (c + 1) * 128, :], x_sb[:, c, :], "xio%d" % c, [xb[c]], [])
                if tl + 1 < n_tiles:
                    dma(x_sb[:, c, :], x_d[(tl + 1) * T + c * 128: (tl + 1) * T + (c + 1) * 128, :], "xio%d" % c, [], [xb[c]])
            return
        if pre["pendT"] is not None:
            emit_T(pre["pendT"])
        ahead_square(c)
        ahead_norm(c)
        pre["pendT"] = c

    def phase_finish():
        if pre["ahead"] and pre["pendT"] is not None:
            emit_T(pre["pendT"])
        pre["pendT"] = None

    def prenorm():
        a = pre["par"]
        for c in range(NCH):
            if not pre["ss"][c]:
                ahead_square(c)
        for c in range(NCH):
            if not pre["norm"][c]:
                ahead_norm(c)
        for c in range(NCH):
            if not pre["T"][c]:
                emit_T(c)
        pre["par"] = a ^ 1
        pre_reset()
        return xnTP[a], xnTP_b[a]

    def postnorm(c, yb, gt, gt_b, half):
        if "pn" not in ph:
            ph["pn"] = (junkP, junkP_b, [T_([D]) for _ in range(2)], [Buf("ptmp0"), Buf("ptmp1")], [0])
        junk, jb, tmps_, tbs_, ctr_ = ph["pn"]
        ssy = T_([1])
        sb_ = Buf("ssy")
        y = ps2(yb)
        act(junk, y, AF.Square, [psb[yb], psb[yb + 1]], [jb, sb_], accum_out=ssy[:, 0:1])
        if half:
            rs, rb = rstd_from(ssy, sb_, 4.0 / D, 4.0 * EPS)
        else:
            rs, rb = rstd_from(ssy, sb_, 1.0 / D, EPS)
        tmp = tmps_[ctr_[0] % len(tmps_)]
        tb = tbs_[ctr_[0] % len(tmps_)]
        ctr_[0] += 1
        stt(tmp, y, rs[:, 0:1], gt, ALU.mult, ALU.mult, [psb[yb], psb[yb + 1], rb, gt_b], [tb])
        tt("dve", x_sb[:, c, :], x_sb[:, c, :], tmp, ALU.add, [xb[c], tb], [xb[c]])
        chunk_final(c)

    def out_proj(c, actT, actT_bufs, nk, half, gt, gt_b, split=None):
        yb = Y_next()
        bufs = list(actT_bufs) if len(actT_bufs) == nk else [actT_bufs[0]] * nk
        mm = lambda hf, k: (ps[yb + hf], actT[:, k, c * 128:(c + 1) * 128], Rs[k][:, hf * 512:(hf + 1) * 512], k == 0, k == nk - 1)
        if split:
            mm_group([mm(0, k) for k in range(split)], list(dict.fromkeys(bufs[:split])) + [R_b[k] for k in range(split)], [psb[yb], psb[yb + 1]])
            mm_group([mm(0, k) for k in range(split, nk)] + [mm(1, k) for k in range(nk)],
                     list(dict.fromkeys(bufs)) + [R_b[k] for k in range(nk)], [psb[yb], psb[yb + 1]])
        else:
            mm_group([mm(hf, k) for hf in range(2) for k in range(nk)], list(dict.fromkeys(bufs)) + [R_b[k] for k in range(nk)],
                     [psb[yb], psb[yb + 1]])
        postnorm(c, yb, gt, gt_b, half)

    def gtab_load(L, which, slot):
        dma(gtab[slot], normg_d.rearrange("(l i k) p -> (l i) (k p)", i=7, k=8)[L * 7 + which: L * 7 + which + 1, :].partition_broadcast(128),
            "gt%d" % slot, [], [gtab_b[slot]])

    def ple_prep_load(L, tile, last):
        p_sb = T_([NCH, 256])
        p_b = Buf("p")
        dma(p_sb, p_d[L, tile * T:(tile + 1) * T, :].rearrange("(c p) f -> p c f", p=128), "pld", [], [p_b], extra=last)
        return p_sb, p_b

    def ple_prep_T(p_sb, p_b):
        pbf = T_([NCH, 256], BF16)
        pbf_b = Buf("pbf")
        P.op("dve", lambda e: e.tensor_copy(out=pbf, in_=p_sb), reads=[p_b], writes=[pbf_b])
        bk = A_next()
        tr_group([(psT[bk][:, (c * 2 + k) * 128:(c * 2 + k + 1) * 128], pbf[:, c, k * 128:(k + 1) * 128], identb)
                  for c in range(NCH) for k in range(2)], [pbf_b, cst2_b], [psb[bk]])
        pT = T_([2, T], BF16)
        pT_b = Buf("pT")
        for c in range(NCH):
            P.op("act", lambda e, c=c, bk=bk: e.activation(out=pT[:, :, c * 128:(c + 1) * 128],
                                                           in_=psT[bk][:, c * 256:(c + 1) * 256].rearrange("p (k t) -> p k t", k=2),
                                                           func=AF.Copy),
                 reads=[psb[bk]], writes=[pT_b])
        return pT, pT_b

    def ffn(L, f, fuse_ple_tile=None):
        last = new_phase()
        fused = None
        if fuse_ple_tile is not None and not bg["on"]:
            fused = {"p": ple_prep_load(L, fuse_ple_tile, last)}
        slot = 0 if f == 0 else 2
        gtab_load(L, 1 if f == 0 else 5, slot)
        xnT, xnT_b = prenorm()
        for j in range(NJ):
            R_load(("ffn", L, f, j), j)
        hT = T_([NJ, T], BF16)
        hT_b = [Buf("hT%d" % j) for j in range(NJ)]
        sg = [T_([T]) for _ in range(2)]
        sg_b = [Buf("sg0"), Buf("sg1")]
        for j in range(NJ):
            w, wb = ring_next(("ffn", L, f, j))
            bgt, bu = A_next(), A_next()
            mm_group([(ps[bgt], w[:, k, 0:128], xnT[:, k, :], k == 0, k == 7) for k in range(8)], [wb] + xnT_b, [psb[bgt]])
            mm_group([(ps[bu], w[:, k, 128:256], xnT[:, k, :], k == 0, k == 7) for k in range(8)], [wb] + xnT_b, [psb[bu]])
            ring_done()
            i = j % 2
            act(sg[i], ps[bgt], AF.Silu, [psb[bgt]], [sg_b[i]])
            tt("dve", hT[:, j, :], ps[bu], sg[i], ALU.mult, [psb[bu], sg_b[i]], [hT_b[j]])
            bg_pump(1)
        if fused is not None:
            fused["pT"] = ple_prep_T(*fused["p"])
        for c in range(NCH):
            out_proj(c, hT, hT_b, NJ, True, gtab[slot], gtab_b[slot], split=(16 if c == 0 else None))
            bg_pump(2)
        if fused is not None:
            for k in range(10):
                R_load(("ple", L, k), k)
        phase_finish()
        return fused

    def ple(L, tile, fused=None):
        if fused is None:
            last = new_phase()
            if L in bg_end and bg["np"] < bg_end[L]:
                bg_require(bg_end[L])
            for k in range(10):
                R_load(("ple", L, k), k)
            p_sb, p_b = ple_prep_load(L, tile, last)
            xnT, xnT_b = prenorm()
            pT, pT_b = ple_prep_T(p_sb, p_b)
        else:
            xnT, xnT_b = prenorm()
            pT, pT_b = fused["pT"]
        sgs = [T_([D]) for _ in range(2)]
        sgs_b = [Buf("sgs0"), Buf("sgs1")]
        tmps = [T_([D]) for _ in range(2)]
        tmps_b = [Buf("tmps0"), Buf("tmps1")]
        for c in range(NCH):
            yg = Y_next()
            mms = []
            for hf in range(2):
                for k in range(8):
                    mms.append((ps[yg + hf], xnT[:, k, c * 128:(c + 1) * 128], Rs[k][:, hf * 512:(hf + 1) * 512], k == 0, k == 7))
            mm_group(mms, xnT_b + [R_b[k] for k in range(8)], [psb[yg], psb[yg + 1]])
            yp = Y_next()
            mms = []
            for hf in range(2):
                for k in range(2):
                    mms.append((ps[yp + hf], pT[:, k, c * 128:(c + 1) * 128], Rs[8 + k][:, hf * 512:(hf + 1) * 512], k == 0, k == 1))
            mm_group(mms, [pT_b, R_b[8], R_b[9]], [psb[yp], psb[yp + 1]])
            i = c % 2
            act(sgs[i], ps2(yg), AF.Sigmoid, [psb[yg], psb[yg + 1]], [sgs_b[i]])
            tt("dve", tmps[i], ps2(yp), sgs[i], ALU.mult, [psb[yp], psb[yp + 1], sgs_b[i]], [tmps_b[i]])
            tt("dve", x_sb[:, c, :], x_sb[:, c, :], tmps[i], ALU.add, [xb[c], tmps_b[i]], [xb[c]])
            chunk_final(c)
            bg_pump(1)
        phase_finish()

    def conv(L):
        jj = L // 3
        new_phase()
        gtab_load(L, 3, 1)
        xnT, xnT_b = prenorm()
        for m in range(8):
            R_load(("mix", L, m), m)
        u = T_([8, T + 2])
        u_b = [Buf("u%d" % m) for m in range(8)]
        zT = T_([8, T], BF16)
        zT_b = [Buf("zTc%d" % m) for m in range(8)]
        Csb = [T_([T]) for _ in range(2)]
        Csb_b = [Buf("Csb0"), Buf("Csb1")]
        acc = [T_([T]) for _ in range(2)]
        acc_b = [Buf("acc0"), Buf("acc1")]
        P.op("act", lambda e: e.activation(out=u[:, :, 0:2], in_=ucarry[jj], func=AF.Copy), reads=[ucarry_b[jj]], writes=u_b)
        wcur = None
        for m in range(8):
            banks = []
            for t in range(3):
                s = 3 * m + t
                if s % 2 == 0:
                    wcur = ring_next(("mix", L, s // 2))
                w, wb = wcur
                hfc = s % 2
                bk = A_next()
                mm_group([(ps[bk], w[:, k, hfc * 128:(hfc + 1) * 128], xnT[:, k, :], k == 0, k == 7) for k in range(8)],
                         [wb] + xnT_b, [psb[bk]])
                if s % 2 == 1:
                    ring_done()
                banks.append(bk)
            bB, bC, bv = banks
            i = m % 2
            act(Csb[i], ps[bC], AF.Copy, [psb[bC]], [Csb_b[i]])
            tt("dve", u[:, m, 2:T + 2], ps[bv], Csb[i], ALU.mult, [psb[bv], Csb_b[i]], [u_b[m]])
            act(acc[i], u[:, m, 2:T + 2], AF.Copy, [u_b[m], cst2_b], [acc_b[i]], scale=cwT[:, jj, 2, m:m + 1])
            stt(acc[i], u[:, m, 1:T + 1], cwT[:, jj, 1, m:m + 1], acc[i], ALU.mult, ALU.add, [u_b[m], acc_b[i], cst2_b], [acc_b[i]])
            stt(acc[i], u[:, m, 0:T], cwT[:, jj, 0, m:m + 1], acc[i], ALU.mult, ALU.add, [u_b[m], acc_b[i], cst2_b], [acc_b[i]])
            tt("dve", zT[:, m, :], ps[bB], acc[i], ALU.mult, [psb[bB], acc_b[i]], [zT_b[m]])
            bg_pump(1)
        P.op("act", lambda e: e.activation(out=ucarry[jj], in_=u[:, :, T:T + 2], func=AF.Copy), reads=u_b, writes=[ucarry_b[jj]])
        ph["pn"] = (junkP, junkP_b, [T_([D])], [Buf("ptmp0")], [0])
        for c in range(NCH):
            out_proj(c, zT, zT_b, 8, False, gtab[1], gtab_b[1], split=(5 if c == 0 else None))
        phase_finish()

    def swa(L, tile):
        last = new_phase("swa")
        gtab_load(L, 3, 1)
        bm_sb = T_([2, 2, 1024])
        bm_b = Buf("bm")
        dma(bm_sb.rearrange("p a b c -> p (a b c)"), bm_d, "bml", [], [bm_b], extra=last)
        xnT, xnT_b = prenorm()
        for c in range(8):
            R_load(("mix", L, c), c)
        qT = T_([8, T], BF16)
        qT_b = Buf("qT")
        kT = T_([T + 128], BF16)
        kT_b = Buf("kT")
        v_sb = T_([5, 128], BF16)
        v_b = Buf("v")
        oT = T_([8, T], BF16)
        oT_b = [Buf("oT%d" % b) for b in range(NCH)]
        P.op("act", lambda e: e.activation(out=kT[:, 0:128], in_=kcarry, func=AF.Copy), reads=[kvcarry_b], writes=[kT_b])
        P.op("act", lambda e: e.activation(out=v_sb[:, 0, :], in_=vcarry, func=AF.Copy), reads=[kvcarry_b], writes=[v_b])
        for i in range(4):
            w, wb = ring_next(("mix", L, i))
            for hfc in range(2):
                c = 2 * i + hfc
                bk = A_next()
                mm_group([(ps[bk], w[:, k, hfc * 128:(hfc + 1) * 128], xnT[:, k, :], k == 0, k == 7) for k in range(8)],
                         [wb] + xnT_b, [psb[bk]])
                act(qT[:, c, :], ps[bk], AF.Copy, [psb[bk]], [qT_b], scale=0.125)
            ring_done()
        w, wb = ring_next(("mix", L, 4))
        bk = A_next()
        mm_group([(ps[bk], w[:, k, 0:128], xnT[:, k, :], k == 0, k == 7) for k in range(8)], [wb] + xnT_b, [psb[bk]])
        act(kT[:, 128:T + 128], ps[bk], AF.Copy, [psb[bk]], [kT_b])
        bk = A_next()
        mms = []
        for b in range(NCH):
            for k in range(8):
                mms.append((ps[bk][:, b * 128:(b + 1) * 128], xnT[:, k, b * 128:(b + 1) * 128], w[:, k, 128:256], k == 0, k == 7))
        mm_group(mms, [wb] + xnT_b, [psb[bk]])
        ring_done()
        P.op("dve", lambda e, bk=bk: e.tensor_copy(out=v_sb[:, 1:5, :], in_=ps[bk].rearrange("p (b d) -> p b d", b=4)),
             reads=[psb[bk]], writes=[v_b])
        P.op("act", lambda e: e.activation(out=kcarry, in_=kT[:, T:T + 128], func=AF.Copy), reads=[kT_b], writes=[kvcarry_b])
        P.op("act", lambda e: e.activation(out=vcarry, in_=v_sb[:, 4, :], func=AF.Copy), reads=[v_b], writes=[kvcarry_b])
        E = [[[T_([T], BF16) for _ in range(2)] for _ in range(2)] for _ in range(3)]
        E_b = [[[Buf("E%d%d%d" % (p_, a_, b_)) for b_ in range(2)] for a_ in range(2)] for p_ in range(3)]
        tmp = [T_([T]) for _ in range(2)]
        tmp_b = [Buf("stmp0"), Buf("stmp1")]
        rden = [T_([1024]) for _ in range(2)]
        rden_b = [[Buf("rden%d%d" % (p_, k_)) for k_ in range(2)] for p_ in range(2)]
        ph["pn"] = (junkP, junkP_b, [T_([D])], [Buf("ptmp0")], [0])
        tcs = [0]
        iters = [(b, kv) for b in range(NCH) for kv in range(2)]

        def kbs_of(b):
            return [1] if (tile == 0 and b == 0) else [0, 1]

        def s1(i):
            b, kv = iters[i]
            par = i % 3
            rows = slice(kv * 64, kv * 64 + 64)
            for kb in kbs_of(b):
                for hf in range(2):
                    bk = A_next()
                    mm_group([(ps[bk], kT[rows, (b + kb) * 128:(b + kb + 1) * 128],
                               qT[rows, hf * 4:(hf + 1) * 4, b * 128:(b + 1) * 128], True, True)],
                             [kT_b, qT_b], [psb[bk]])
                    ti = tcs[0] % 2
                    tcs[0] += 1
                    tt("dve", tmp[ti], ps[bk], bm_sb[:, kv, kb, hf * 512:(hf + 1) * 512], ALU.add, [psb[bk], bm_b], [tmp_b[ti]])
                    act(E[par][kb][hf], tmp[ti], AF.Exp, [tmp_b[ti]], [E_b[par][kb][hf]])

        def s2a(i):
            b, kv = iters[i]
            par = i % 3
            kbs = kbs_of(b)
            rows = slice(kv * 64, kv * 64 + 64)
            Ei, Ei_b = E[par], E_b[par]
            rd, rd_b = rden[(i // 2) % 2], rden_b[(i // 2) % 2][kv]
            yd = Y_next()
            mms = []
            for hf in range(2):
                for kb in kbs:
                    mms.append((ps[yd + hf], onesb, Ei[kb][hf], kb == kbs[0], kb == kbs[-1]))
            mm_group(mms, [cst2_b] + [Ei_b[kb][hf] for kb in kbs for hf in range(2)], [psb[yd], psb[yd + 1]])
            es = expsink[rows, kv * 8:(kv + 1) * 8].unsqueeze(2).to_broadcast([64, 8, 128])
            tt("dve", rd[rows, :].rearrange("p (g q) -> p g q", g=8), ps2(yd)[rows, :].rearrange("p (g q) -> p g q", g=8), es,
               ALU.add, [psb[yd], psb[yd + 1], cst2_b], [rd_b])
            act(rd[rows, :], rd[rows, :], AF.Ln, [rd_b], [rd_b])
            act(rd[rows, :], rd[rows, :], AF.Exp, [rd_b], [rd_b], scale=-1.0)

        def s2b(i):
            b, kv = iters[i]
            par = i % 3
            kbs = kbs_of(b)
            rows = slice(kv * 64, kv * 64 + 64)
            Ei, Ei_b = E[par], E_b[par]
            rd, rd_b = rden[(i // 2) % 2], rden_b[(i // 2) % 2][kv]
            yo = Y_next()
            mms = []
            for hf in range(2):
                for kb in kbs:
                    mms.append((ps[yo + hf], v_sb[:, b + kb, :], Ei[kb][hf], kb == kbs[0], kb == kbs[-1]))
            mm_group(mms, [v_b] + [Ei_b[kb][hf] for kb in kbs for hf in range(2)], [psb[yo], psb[yo + 1]])
            tt("dve", oT[rows, :, b * 128:(b + 1) * 128], ps2(yo)[rows, :].rearrange("p (g q) -> p g q", g=8),
               rd[rows, :].rearrange("p (g q) -> p g q", g=8), ALU.mult, [psb[yo], psb[yo + 1], rd_b], [oT_b[b]])

        n_it = len(iters)
        s1(0)
        s1(1)
        s2a(0)
        for i in range(n_it):
            if i + 2 < n_it:
                s1(i + 2)
            if i + 1 < n_it:
                s2a(i + 1)
            s2b(i)
            if iters[i][1] == 1:
                b_ = iters[i][0]
                out_proj(b_, oT, [oT_b[b_]], 8, False, gtab[1], gtab_b[1])
        phase_finish()

    def ret_tok0(L):
        last = new_phase("tok0")
        a = pre["par"]
        if not pre["ss"][0]:
            ahead_square(0)
        if not pre["norm"][0]:
            ahead_norm(0)
        xn32 = T_([D])
        xn32_b = Buf("xn32")
        xh = T_([D], BF16)
        xl = T_([D], BF16)
        xhl_b = Buf("xhl")
        xhT = T_([8, 128], BF16)
        xlT = T_([8, 128], BF16)
        xT_b = Buf("xhlT")
        wst = T_([8, 512])
        wst_b = Buf("wst")
        whi = T_([8, 512], BF16)
        wlo = T_([8, 512], BF16)
        whl_b = Buf("whl")
        qsb = T_([1024])
        qsb_b = Buf("qsb")
        prod = T_([512])
        prod_b = Buf("prod")
        gi = L * 7 + 2
        P.op("dve", lambda e: e.tensor_scalar(out=xn32, in0=x_sb[:, 0, :], scalar1=rstdP[a][:, 0:1], scalar2=None, op0=ALU.mult),
             reads=[xb[0], rstdP_b[a][0]], writes=[xn32_b])
        act(xh, xn32, AF.Copy, [xn32_b], [xhl_b])
        tt("dve", xl, xn32, xh, ALU.subtract, [xn32_b, xhl_b], [xhl_b])
        for (src, dst) in ((xh, xhT), (xl, xlT)):
            bk = A_next()
            tr_group([(psT[bk][:, k * 128:(k + 1) * 128], src[:, k * 128:(k + 1) * 128], identb) for k in range(8)],
                     [xhl_b, cst2_b], [psb[bk]])
            P.op("dve", lambda e, bk=bk, dst=dst: e.tensor_copy(out=dst, in_=psT[bk].rearrange("p (k t) -> p k t", k=8)),
                 reads=[psb[bk]], writes=[xT_b])
        wsrc = rqkvg_d[0]
        gb = gT[:, gi, :].unsqueeze(2).to_broadcast([128, 8, 512])
        for cb in range(4):
            dma(wst, wsrc[:, cb * 512:(cb + 1) * 512].rearrange("(k p) c -> p k c", p=128), "t0w", [], [wst_b], extra=last)
            tt("dve", wst, wst, gb, ALU.mult, [wst_b, cst2_b], [wst_b])
            act(whi.rearrange("p a b -> p (a b)"), wst.rearrange("p a b -> p (a b)"), AF.Copy, [wst_b], [whl_b])
            tt("dve", wlo, wst, whi, ALU.subtract, [wst_b, whl_b], [whl_b])
            bank = A_next()
            mms = []
            combos = [(xhT, whi), (xhT, wlo), (xlT, whi)]
            for ci, (xa, wa) in enumerate(combos):
                for k in range(8):
                    mms.append((ps[bank], xa[:, k, :], wa[:, k, :], ci == 0 and k == 0, ci == 2 and k == 7))
            mm_group(mms, [xT_b, whl_b], [psb[bank]])
            if cb < 2:
                act(qsb[:, cb * 512:(cb + 1) * 512], ps[bank], AF.Copy, [psb[bank]], [qsb_b])
            else:
                tt("dve", prod, ps[bank], qsb[:, (cb - 2) * 512:(cb - 1) * 512], ALU.mult, [psb[bank], qsb_b], [prod_b])
                for hh in range(2):
                    h = (cb - 2) * 2 + hh
                    act(junkP[:, 0:256], prod[:, hh * 256:(hh + 1) * 256], AF.Copy, [prod_b], [junkP_b, i00_b],
                        accum_out=i00P[:, h:h + 1])

    def ret(L, tile):
        last = new_phase("ret")
        gtab_load(L, 3, 1)
        cs = T_([2, T])
        cs_b = Buf("cs")
        dma(cs, rot_d[0:2, :, tile * T:(tile + 1) * T].rearrange("a p t -> p a t"), "rot0", [], [cs_b], extra=last)
        csq = [T_([2, T])]
        csq_b = [Buf("csq0")]
        xnT, xnT_b = prenorm()
        for kk in range(16):
            R_load(("mix", L, kk), kk)
        zT = T_([16, T], BF16)
        zT_b = [Buf("zTr%d" % c) for c in range(NCH)]
        ph["pn"] = (junkP, junkP_b, [T_([D])], [Buf("ptmp0")], [0])
        qdT = T_([2, T], BF16)
        qdT_b = Buf("qdT")
        kT = T_([2, T], BF16)
        kT_b = Buf("kTr")
        v_h = T_([NCH, 512], BF16)
        v_hb = Buf("v_h")
        sg_h = T_([NCH, 512], BF16)
        sg_hb = Buf("sg_h")
        tq = [T_([T]) for _ in range(2)]
        tq_b = [Buf("tq%d" % i) for i in range(2)]
        kd_sb = [T_([256], BF16) for _ in range(NCH)]
        kd_b = [Buf("kd%d" % i) for i in range(NCH)]
        inn = [T_([128], BF16) for _ in range(NCH)]
        inn_b = [Buf("inn%d" % i) for i in range(NCH)]
        stbf = [T_([2, 512], BF16) for _ in range(2)]
        stbf_b = [Buf("stbf0"), Buf("stbf1")]
        stats = [T_([6]) for _ in range(2)]
        stats_b = [Buf("stats0"), Buf("stats1")]
        mv = [T_([2]) for _ in range(2)]
        mv_b = [Buf("mv0"), Buf("mv1")]
        tno = [T_([512]) for _ in range(2)]
        tno_b = [Buf("tno0"), Buf("tno1")]
        z = [T_([512], BF16) for _ in range(2)]
        z_b = [Buf("z0"), Buf("z1")]
        for h in range(4):
            hs = 0
            dma(csq[hs], rot_d[2 + 2 * h:4 + 2 * h, :, tile * T:(tile + 1) * T].rearrange("a p t -> p a t"), "rotq%d" % hs, [], [csq_b[hs]],
                extra=last)

            def rotary(key, tab, tab_b, outT, outT_b):
                w, wb = ring_next(key)
                b0, b1 = A_next(), A_next()
                mm_group([(ps[b0], w[:, k, 0:128], xnT[:, k, :], k == 0, k == 7) for k in range(8)], [wb] + xnT_b, [psb[b0]])
                mm_group([(ps[b1], w[:, k, 128:256], xnT[:, k, :], k == 0, k == 7) for k in range(8)], [wb] + xnT_b, [psb[b1]])
                ring_done()
                tt("dve", tq[0], ps[b0], tab[:, 0, :], ALU.mult, [psb[b0], tab_b], [tq_b[0]])
                tt("dve", tq[1], ps[b1], tab[:, 1, :], ALU.mult, [psb[b1], tab_b], [tq_b[1]])
                tt("pool", outT[:, 0, :], tq[0], tq[1], ALU.subtract, [tq_b[0], tq_b[1]], [outT_b])
                tt("dve", tq[0], ps[b0], tab[:, 1, :], ALU.mult, [psb[b0], tab_b], [tq_b[0]])
                tt("dve", tq[1], ps[b1], tab[:, 0, :], ALU.mult, [psb[b1], tab_b], [tq_b[1]])
                tt("pool", outT[:, 1, :], tq[0], tq[1], ALU.add, [tq_b[0], tq_b[1]], [outT_b])

            rotary(("mix", L, 6 * h + 0), csq[hs], csq_b[hs], qdT, qdT_b)
            rotary(("mix", L, 6 * h + 1), cs, cs_b, kT, kT_b)
            for hv in range(2):
                w, wb = ring_next(("mix", L, 6 * h + 2 + hv))
                for c in range(NCH):
                    bk = A_next()
                    mm_group([(ps[bk][:, 0:256], xnT[:, k, c * 128:(c + 1) * 128], w[:, k, :], k == 0, k == 7) for k in range(8)],
                             [wb] + xnT_b, [psb[bk]])
                    act(v_h[:, c, hv * 256:(hv + 1) * 256], ps[bk][:, 0:256], AF.Copy, [psb[bk]], [v_hb])
                ring_done()
            for hg in range(2):
                w, wb = ring_next(("mix", L, 6 * h + 4 + hg))
                for c in range(NCH):
                    bk = A_next()
                    mm_group([(ps[bk][:, 0:256], xnT[:, k, c * 128:(c + 1) * 128], w[:, k, :], k == 0, k == 7) for k in range(8)],
                             [wb] + xnT_b, [psb[bk]])
                    act(sg_h[:, c, hg * 256:(hg + 1) * 256], ps[bk][:, 0:256], AF.Silu, [psb[bk]], [sg_hb])
                ring_done()
            for c in range(NCH):
                csl = slice(c * 128, (c + 1) * 128)
                bk = A_next()
                tr_group([(psT[bk][:, e * 128:(e + 1) * 128], kT[:, e, csl], identb) for e in range(2)], [kT_b, cst2_b], [psb[bk]])
                act(kd_sb[c], psT[bk][:, 0:256], AF.Copy, [psb[bk], cst_b], [kd_b[c]], scale=skd[:, h:h + 1])
                bi = A_next()
                mm_group([(ps[bi][:, 0:128], kT[:, e, csl], qdT[:, e, csl], e == 0, e == 1) for e in range(2)], [kT_b, qdT_b], [psb[bi]])
                dm_ = dmask0 if (tile == 0 and c == 0) else dmask
                tt("dve", inn[c], ps[bi][:, 0:128], dm_[:, h, :], ALU.mult, [psb[bi], cst_b], [inn_b[c]])
                if tile == 0 and c == 0:
                    stt(inn[0], e00, i00P[:, h:h + 1], inn[0], ALU.mult, ALU.add, [cst_b, i00_b, inn_b[0]], [inn_b[0]])

            def stage_a(c, h=h):
                gc = tile * NCH + c
                i2 = c % 2
                csl = slice(c * 128, (c + 1) * 128)
                if gc > 0:
                    act(stbf[i2].rearrange("p a b -> p (a b)"), state[:, h, :, :].rearrange("p a b -> p (a b)"), AF.Copy,
                        [state_b[h][0], state_b[h][1]], [stbf_b[i2]])
                yd = Y_next()
                mm_group([(ps[yd + e], kd_sb[c][:, e * 128:(e + 1) * 128], v_h[:, c, :], True, True) for e in range(2)],
                         [kd_b[c], v_hb], [psb[yd], psb[yd + 1]])
                for e in range(2):
                    stt(state[:, h, e, :], state[:, h, e, :], gC[h], ps[yd + e], ALU.mult, ALU.add,
                        [state_b[h][e], psb[yd + e]], [state_b[h][e]])
                bo = A_next()
                mms = [(ps[bo], inn[c], v_h[:, c, :], True, gc == 0)]
                rd = [inn_b[c], v_hb]
                if gc > 0:
                    for e in range(2):
                        mms.append((ps[bo], qdT[:, e, csl], stbf[i2][:, e, :], False, e == 1))
                    rd += [qdT_b, stbf_b[i2]]
                mm_group(mms, rd, [psb[bo]])
                P.op("dve", lambda e, bo=bo, i2=i2: e.bn_stats(out=stats[i2], in_=ps[bo]), reads=[psb[bo]], writes=[stats_b[i2]])
                P.op("dve", lambda e, i2=i2: e.bn_aggr(out=mv[i2], in_=stats[i2]), reads=[stats_b[i2]], writes=[mv_b[i2]])
                rs, rb = rstd_from(mv[i2][:, 1:2], mv_b[i2], 1.0, EPS)
                ts("dve", tno[i2], ps[bo], mv[i2][:, 0:1], rs[:, 0:1], ALU.subtract, ALU.mult, [psb[bo], mv_b[i2], rb], [tno_b[i2]])
                tt("pool", z[i2], tno[i2], sg_h[:, c, :], ALU.mult, [tno_b[i2], sg_hb], [z_b[i2]])

            def stage_b(c, h=h):
                i2 = c % 2
                csl = slice(c * 128, (c + 1) * 128)
                bz = A_next()
                tr_group([(psT[bz][:, i * 128:(i + 1) * 128], z[i2][:, i * 128:(i + 1) * 128], identb) for i in range(4)],
                         [z_b[i2], cst2_b], [psb[bz]])
                P.op("act", lambda e, bz=bz, h=h, csl=csl: e.activation(out=zT[:, h * 4:(h + 1) * 4, csl],
                                                                       in_=psT[bz][:, 0:512].rearrange("p (i t) -> p i t", i=4), func=AF.Copy),
                     reads=[psb[bz]], writes=[zT_b[c]])
                if h == 3:
                    out_proj(c, zT, [zT_b[c]], 16, False, gtab[1], gtab_b[1])

            stage_a(0)
            stage_a(1)
            stage_b(0)
            stage_a(2)
            stage_b(1)
            stage_a(3)
            stage_b(2)
            stage_b(3)
        phase_finish()

    for tile in range(n_tiles):
        pre_reset()
        if tile == 0:
            for c in range(NCH):
                dma(x_sb[:, c, :], x_d[c * 128:(c + 1) * 128, :], "xio%d" % c, [], [xb[c]])
        fused_state = None
        plan = []
        for L in layers:
            kind = L % 3
            if "f0" in subs:
                plan.append(("f0", L))
            if "mix" in subs:
                plan.append(("mix", L))
            if "f1" in subs:
                plan.append(("f1", L))
            if "ple" in subs:
                plan.append(("ple", L))
        if tile == 1 and bg["np"] < len(bgl):
            bg_require(len(bgl))
        for pi, (what, L) in enumerate(plan):
            kind = L % 3
            pre["ahead"] = pi + 1 < len(plan)
            pre["io"] = None if pre["ahead"] else tile
            if what == "f0":
                ffn(L, 0)
            elif what == "f1":
                nxt_ple = pi + 1 < len(plan) and plan[pi + 1] == ("ple", L)
                fused_state = ffn(L, 1, fuse_ple_tile=(tile if nxt_ple else None))
            elif what == "ple":
                ple(L, tile, fused=(fused_state if (pi > 0 and plan[pi - 1] == ("f1", L)) else None))
            elif kind == 0:
                conv(L)
            elif kind == 1:
                swa(L, tile)
            else:
                if tile == 0:
                    ret_tok0(L)
                ret(L, tile)
    P.emit()
    return nc


_CONSTS = None


def _prep_inputs(inputs, b, consts, ntok=SEQ):
    f = lambda a: np.ascontiguousarray(np.asarray(a, dtype=np.float32))
    m = {
        "x": f(inputs["x"][b][:ntok]),
        "p": f(inputs["p"][:, b, :ntok]),
        "norm_g": f(inputs["norm_g"]).reshape(DEPTH * 7 * 8, 128),
        "ffn_w_gu": f(inputs["ffn_w_gu"]),
        "ffn_w_down": f(inputs["ffn_w_down"]),
        "ple_w_proj": f(inputs["ple_w_proj"]),
        "ple_w_gate": f(inputs["ple_w_gate"]),
        "conv_w_in": f(inputs["conv_w_in"]),
        "conv_w": f(inputs["conv_w"]).reshape(48, 128),
        "conv_w_out": f(inputs["conv_w_out"]),
        "swa_w_qkv": f(inputs["swa_w_qkv"]),
        "swa_sinks": f(inputs["swa_sinks"]),
        "swa_w_o": f(inputs["swa_w_o"]),
        "ret_w_qkvg": f(inputs["ret_w_qkvg"]),
        "ret_w_o": f(inputs["ret_w_o"]),
        "c_bias": consts["bias"],
        "c_rot": consts["rot"],
        "c_dmask": consts["dmask"].reshape(128, 512),
        "c_dmask0": consts["dmask0"].reshape(128, 512),
        "c_e00": consts["e00"],
        "c_skd": consts["skd"],
        "c_ident": consts["ident"],
    }
    return m


def kernel(**inputs):
    global _CONSTS
    if _CONSTS is None:
        _CONSTS = _host_consts()
    consts = dict(_CONSTS)
    consts["bias"] = _bias_table(np.asarray(inputs["rel_bias"], dtype=np.float32))
    nc = build(8, (0, 1, 2, 3), consts)
    in_maps = [_prep_inputs(inputs, b, consts) for b in range(8)]
    res = run_bass_kernel_spmd(nc, in_maps, core_ids=list(range(8)))
    out = np.stack([np.asarray(r["out"], dtype=np.float32) for r in res.results], axis=0)
    return out
```
